# Optimizing a Trainium2 kernel written in Bass

```python
import jax, jax.numpy as jnp
from jax import lax
import numpy as np

D_MODEL = 1024
BATCH = 8
SEQ = 2048
DEPTH = 2
DEC_BATCH = 128
DEC_SEQ = 4
PAST_LEN = 16384
PAGE_SIZE = 128

EXPAND = 2
E_A = EXPAND * D_MODEL
HEAD_K = 128
N_HEADS_A = E_A // HEAD_K
HEAD_V = E_A // N_HEADS_A
CHUNK_A = 64
E_B = EXPAND * D_MODEL
CHUNK_B = 128
GROUP_B = 128
N_GROUPS_B = E_B // GROUP_B
N_A_LAYERS = (DEPTH + 1) // 2
N_B_LAYERS = DEPTH // 2
ALPHA = (2 * DEPTH) ** 0.25
BETA = (8 * DEPTH) ** -0.25
LN_EPS = 1e-5

kernel_name = "hgrn2_chunk_gmlp_hybrid_step"


def layer_norm(x, g, b):
    xf = x.astype(jnp.float32)
    mu = jnp.mean(xf, axis=-1, keepdims=True)
    var = jnp.mean(jnp.square(xf - mu), axis=-1, keepdims=True)
    y = (xf - mu) * lax.rsqrt(var + LN_EPS) * g.astype(jnp.float32) + b.astype(jnp.float32)
    return y.astype(x.dtype)


def hgrn2_recurrence(q, log_f, v, s0):
    bsz, seq_len, n_heads, _ = q.shape
    pad = (-seq_len) % CHUNK_A
    n_chunks = (seq_len + pad) // CHUNK_A

    def to_chunks(t):
        t = jnp.pad(t.astype(jnp.float32), ((0, 0), (0, pad), (0, 0), (0, 0)))
        return t.reshape(bsz, n_chunks, CHUNK_A, n_heads, t.shape[-1]).transpose(1, 0, 3, 2, 4)

    qc, lfc, vc = to_chunks(q), to_chunks(log_f), to_chunks(v)
    causal = jnp.tril(jnp.ones((CHUNK_A, CHUNK_A), dtype=bool))

    def step(s, inp):
        qn, lfn, vn = inp
        kn = -jnp.expm1(lfn)
        cum = jnp.cumsum(lfn, axis=2)
        cum_last = cum[:, :, -1:, :]
        q_dec = qn * jnp.exp(cum)
        k_inv = kn * jnp.exp(-cum)
        scores = jnp.where(causal, jnp.einsum('bhtk,bhsk->bhts', q_dec, k_inv), 0.0)
        o = jnp.einsum('bhts,bhsv->bhtv', scores, vn) + jnp.einsum('bhtk,bhkv->bhtv', q_dec, s)
        k_end = kn * jnp.exp(cum_last - cum)
        s_new = jnp.exp(cum_last[:, :, 0, :])[..., None] * s + jnp.einsum('bhsk,bhsv->bhkv', k_end, vn)
        return s_new, o

    s_final, o = lax.scan(step, s0.astype(jnp.float32), (qc, lfc, vc))
    o = o.transpose(1, 0, 3, 2, 4).reshape(bsz, n_chunks * CHUNK_A, n_heads, -1)[:, :seq_len]
    return o, s_final


def hgrn2_branch(x, s0, w_in, lb, gnorm, w_out):
    bsz, seq_len, _ = x.shape
    proj = x @ w_in
    q, f_logit, i_in, gate = jnp.split(proj, 4, axis=-1)
    q = jax.nn.silu(q).reshape(bsz, seq_len, N_HEADS_A, HEAD_K)
    log_f = jnp.logaddexp(jnp.log(lb), jnp.log1p(-lb) + jax.nn.log_sigmoid(f_logit.astype(jnp.float32)))
    log_f = log_f.reshape(bsz, seq_len, N_HEADS_A, HEAD_K)
    v = i_in.reshape(bsz, seq_len, N_HEADS_A, HEAD_V)
    o, s_new = hgrn2_recurrence(q, log_f, v, s0)
    o = o * lax.rsqrt(jnp.mean(jnp.square(o), axis=-1, keepdims=True) + LN_EPS)
    o = o * gnorm.astype(jnp.float32).reshape(N_HEADS_A, HEAD_V)
    o = o.reshape(bsz, seq_len, E_A).astype(x.dtype) * jax.nn.silu(gate)
    return o @ w_out, s_new.astype(s0.dtype)


def chunk_gmlp_branch(x, w_in, lnv_g, lnv_b, w_s, b_s, w_out):
    bsz, seq_len, _ = x.shape
    proj = x @ w_in
    u, v = jnp.split(jax.nn.gelu(proj[..., :2 * E_B]), 2, axis=-1)
    z = proj[..., 2 * E_B:]
    v = layer_norm(v, lnv_g, lnv_b)
    pad = (-seq_len) % CHUNK_B
    n_chunks = (seq_len + pad) // CHUNK_B
    vc = jnp.pad(v, ((0, 0), (0, pad), (0, 0))).reshape(bsz, n_chunks, CHUNK_B, N_GROUPS_B, GROUP_B)
    w_causal = jnp.where(jnp.tril(jnp.ones((CHUNK_B, CHUNK_B), dtype=bool)), w_s, 0.0)
    mixed = jnp.einsum('gts,bnsgc->bntgc', w_causal, vc) + b_s.T[None, None, :, :, None]
    mixed = mixed.reshape(bsz, n_chunks * CHUNK_B, E_B)[:, :seq_len]
    out = u * mixed * jax.nn.silu(z)
    return out @ w_out, v


def setup_inputs(seed: int = 0) -> dict:
    key = jax.random.key(seed)
    ks = jax.random.split(key, 16)
    nrm = jax.random.normal
    f32 = jnp.float32
    return {
        "x_prompt": nrm(ks[0], (BATCH, SEQ, D_MODEL), f32),
        "x_sample": nrm(ks[1], (DEC_BATCH, DEC_SEQ, D_MODEL), f32),
        "state_hgrn": 0.5 * nrm(ks[2], (N_A_LAYERS, DEC_BATCH, N_HEADS_A, HEAD_K, HEAD_V), f32),
        "w_in_a": nrm(ks[3], (N_A_LAYERS, D_MODEL, 4 * E_A), f32) * D_MODEL ** -0.5,
        "lb_logits_a": 0.1 * nrm(ks[4], (N_A_LAYERS + 1, N_HEADS_A * HEAD_K), f32),
        "gnorm_a": 1.0 + 0.02 * nrm(ks[5], (N_A_LAYERS, E_A), f32),
        "w_out_a": nrm(ks[6], (N_A_LAYERS, E_A, D_MODEL), f32) * (E_A ** -0.5 * BETA),
        "w_in_b": nrm(ks[7], (N_B_LAYERS, D_MODEL, 3 * E_B), f32) * D_MODEL ** -0.5,
        "lnv_g_b": 1.0 + 0.02 * nrm(ks[8], (N_B_LAYERS, E_B), f32),
        "lnv_b_b": 0.02 * nrm(ks[9], (N_B_LAYERS, E_B), f32),
        "w_s_b": nrm(ks[10], (N_B_LAYERS, N_GROUPS_B, CHUNK_B, CHUNK_B), f32) * CHUNK_B ** -0.5,
        "b_s_b": 1.0 + 0.02 * nrm(ks[11], (N_B_LAYERS, N_GROUPS_B, CHUNK_B), f32),
        "w_out_b": nrm(ks[12], (N_B_LAYERS, E_B, D_MODEL), f32) * (E_B ** -0.5 * BETA),
        "ln_g": 1.0 + 0.02 * nrm(ks[13], (DEPTH, D_MODEL), f32),
        "ln_b": 0.02 * nrm(ks[14], (DEPTH, D_MODEL), f32),
    }


def reference(x_prompt, x_sample, state_hgrn, w_in_a, lb_logits_a, gnorm_a, w_out_a, w_in_b, lnv_g_b,
              lnv_b_b, w_s_b, b_s_b, w_out_b, ln_g, ln_b):
    lb_all = jnp.cumsum(jax.nn.softmax(lb_logits_a.astype(jnp.float32), axis=0), axis=0)
    hp, hs = x_prompt, x_sample
    hgrn_prompt, hgrn_sample, chunk_v_sample = [], [], []
    for layer in range(DEPTH):
        j = layer // 2
        if layer % 2 == 0:
            s_zero = jnp.zeros((hp.shape[0], N_HEADS_A, HEAD_K, HEAD_V), state_hgrn.dtype)
            dp, sp = hgrn2_branch(hp, s_zero, w_in_a[j], lb_all[j], gnorm_a[j], w_out_a[j])
            ds, ss = hgrn2_branch(hs, state_hgrn[j], w_in_a[j], lb_all[j], gnorm_a[j], w_out_a[j])
            hgrn_prompt.append(sp)
            hgrn_sample.append(ss)
        else:
            dp, _ = chunk_gmlp_branch(hp, w_in_b[j], lnv_g_b[j], lnv_b_b[j], w_s_b[j], b_s_b[j], w_out_b[j])
            ds, vs = chunk_gmlp_branch(hs, w_in_b[j], lnv_g_b[j], lnv_b_b[j], w_s_b[j], b_s_b[j], w_out_b[j])
            chunk_v_sample.append(vs)
        hp = layer_norm(ALPHA * hp + dp, ln_g[layer], ln_b[layer])
        hs = layer_norm(ALPHA * hs + ds, ln_g[layer], ln_b[layer])
    return (hp, hs, jnp.stack(hgrn_prompt), jnp.stack(hgrn_sample), jnp.stack(chunk_v_sample))
```

```python
import math
from contextlib import ExitStack

import numpy as np
import concourse.bass as bass
import concourse.mybir as mybir
from concourse.bass_utils import run_bass_kernel_spmd

F32 = mybir.dt.float32
BF16 = mybir.dt.bfloat16
AF = mybir.ActivationFunctionType
ALU = mybir.AluOpType

NCORES = 8
D = 1024
SEQ = 2048
NS = 16
NSTOK = 64
H = 16
ALPHA = 4.0 ** 0.25
EPS = 1e-5
LN_HALF = math.log(0.5)
GC = 0.7978845608028654


def MM(t, out, lhsT, rhs, start, stop):
    return t.matmul(out, lhsT=lhsT, rhs=rhs, start=start, stop=stop, skip_group_check=True)


class Buf:
    def __init__(self, name, t):
        self.name = name
        self.t = t
        self.lw = None
        self.rd = {}
        self.dsem = None
        self.dcnt = 0

    def __getitem__(self, k):
        return self.t[k]


class Alias(Buf):
    def __init__(self, base, t):
        self.base = base
        self.name = base.name
        self.t = t

    lw = property(lambda s: s.base.lw, lambda s, v: setattr(s.base, "lw", v))
    rd = property(lambda s: s.base.rd, lambda s, v: setattr(s.base, "rd", v))
    dsem = property(lambda s: s.base.dsem, lambda s, v: setattr(s.base, "dsem", v))
    dcnt = property(lambda s: s.base.dcnt, lambda s, v: setattr(s.base, "dcnt", v))


class Sched:
    EPOCH = 3000

    def __init__(self, nc, es):
        self.nc = nc
        self.es = es
        self.eng = {"pe": nc.tensor, "act": nc.scalar, "dve": nc.vector, "pool": nc.gpsimd, "sp": nc.sync}
        self.sem = {}
        self.cnt = {}
        self.nsem = 0
        for e in ("pe", "act", "dve", "pool"):
            self.sem[e] = self._newsem(e)
            self.cnt[e] = 0
        self.known = {e: {} for e in self.eng}
        self.pend_r = []
        self.pend_w = []
        self.dma_toks = []
        self.out_toks = []
        self.ninst = 0

    def _newsem(self, tag):
        self.nsem += 1
        return self.es.enter_context(self.nc.semaphore("s%s%d" % (tag, self.nsem)))

    def _wait(self, e, toks):
        k = self.known[e]
        for tok in toks:
            if tok is None:
                continue
            sem, val, src = tok
            if e == "pe" and src == "pe":
                continue
            key = id(sem)
            if k.get(key, 0) >= val:
                continue
            self.eng[e].wait_ge(sem, val)
            k[key] = val

    def _deps(self, reads, writes):
        toks = []
        for b in reads:
            toks.append(b.lw)
        for b in writes:
            toks.append(b.lw)
            toks.extend(b.rd.values())
        return toks

    def _mark(self, tok, reads, writes):
        key = id(tok[0])
        for b in reads:
            b.rd[key] = tok
        for b in writes:
            b.lw = tok
            b.rd = {}

    def op(self, e, fn, reads=(), writes=(), inc=True):
        self._wait(e, self._deps(reads, writes))
        ins = fn(self.eng[e])
        self.ninst += 1
        if e == "pe" and not inc:
            self.pend_r.extend(reads)
            self.pend_w.extend(writes)
            return
        if self.cnt[e] >= self.EPOCH:
            self.sem[e] = self._newsem(e)
            self.cnt[e] = 0
        self.cnt[e] += 1
        ins.then_inc(self.sem[e], 1)
        tok = (self.sem[e], self.cnt[e], e)
        if e == "pe":
            reads = list(reads) + self.pend_r
            writes = list(writes) + self.pend_w
            self.pend_r, self.pend_w = [], []
        self._mark(tok, reads, writes)

    def dma(self, q, out, in_, sbuf, load, is_out=False):
        if load:
            self._wait(q, self._deps((), (sbuf,)))
        else:
            self._wait(q, self._deps((sbuf,), ()))
        if sbuf.dsem is None:
            sbuf.dsem = self._newsem("d")
        ins = self.eng[q].dma_start(out=out, in_=in_)
        sbuf.dcnt += 16
        ins.then_inc(sbuf.dsem, 16)
        tok = (sbuf.dsem, sbuf.dcnt, "dma")
        if load:
            self._mark(tok, (), (sbuf,))
        else:
            self._mark(tok, (sbuf,), ())
        self.dma_toks.append(tok)
        if is_out:
            self.out_toks.append(tok)

    def barrier(self):
        assert not self.pend_r and not self.pend_w
        toks = [(self.sem[e], self.cnt[e], "x") for e in ("pe", "act", "dve", "pool") if self.cnt[e] > 0]
        last = {}
        for t in self.dma_toks:
            last[id(t[0])] = t
        toks += list(last.values())
        for e in self.eng:
            k = self.known[e]
            for sem, val, _ in toks:
                if k.get(id(sem), 0) >= val:
                    continue
                self.eng[e].wait_ge(sem, val)
                k[id(sem)] = val
        self.dma_toks = []

    def finish(self):
        self._wait("sp", self.out_toks)


def build_nc():
    nc = bass.Bass("TRN2", target_bir_lowering=False)

    def din(name, shape):
        return nc.dram_tensor(name, list(shape), F32, kind="ExternalInput").ap()

    def dout(name, shape):
        return nc.dram_tensor(name, list(shape), F32, kind="ExternalOutput").ap()

    xp = din("xp", [SEQ, D])
    xs = din("xs", [NSTOK, D])
    s0 = din("s0", [H, 128, NS * 128])
    wia = din("wia", [H, 128, 8 * 512])
    woa = din("woa", [128, 16 * 1024])
    wuz = din("wuz", [16, 128, 8 * 256])
    wv = din("wv", [128, 8 * 2048])
    wob = din("wob", [128, 16 * 1024])
    wst = din("wst", [128, 16 * 128])
    wsm = din("wsm", [64, 16 * 64])
    bs = din("bs", [1, 16 * 128])
    bss = din("bss", [1, 16 * 64])
    lbl = din("lbl", [2, 128, 16])
    gnm = din("gnm", [128, 16])
    lvg = din("lvg", [1, 2048])
    lvb = din("lvb", [1, 2048])
    lvbc = din("lvbc", [1, 2048])
    lng = din("lng", [2, 1, 1024])
    lnb = din("lnb", [2, 1, 1024])
    yp = dout("yp", [SEQ, D])
    ys = dout("ys", [NSTOK, D])
    spo = dout("spo", [H, 128, 128])
    sso = dout("sso", [H, 128, NS * 128])
    cvo = dout("cvo", [NSTOK, 2048])

    with ExitStack() as es:
        S = Sched(nc, es)

        def sb(name, shape, dt):
            return Buf(name, es.enter_context(nc.sbuf_tensor(name, list(shape), dt)))

        XT = [sb("XT%d" % i, [128, 8, 512 if i < 2 else 64], BF16) for i in range(3)]
        OT = [[sb("OT%d_%d" % (h, i), [128, 512 if i < 2 else 64], BF16) for i in range(3)] for h in range(H)]
        WH = [sb("WH%d" % i, [128, 8 * 512], BF16) for i in range(2)]
        WBIG = sb("WBIG", [128, 16 * 1024], BF16)
        SC = [sb("SC%d" % h, [128, 128], F32) for h in range(H)]
        XIN = [sb("XIN%d" % i, [128, 1024], F32) for i in range(2)]
        identF = sb("identF", [128, 128], F32)
        identB = sb("identB", [128, 128], BF16)
        onesB = sb("onesB", [128, 128], BF16)
        maskP = sb("maskP", [128, 128], F32)
        maskS = sb("maskS", [64, 64], F32)
        maskU = sb("maskU", [128, 128], F32)
        rowsel = sb("rowsel", [64, 16], F32)
        rm = sb("rm", [128, 512], F32)
        rmS = sb("rmS", [128, 64], F32)
        lbc = sb("lbc", [128, 6 * 16], F32)
        gnc = sb("gnc", [128, 16], F32)
        WST = sb("WST", [128, 16 * 128], BF16)
        WSM = sb("WSM", [64, 16 * 64], BF16)
        LNG = sb("LNG", [128, 1024], F32)
        LNB = sb("LNB", [128, 1024], F32)
        ARENA = sb("ARENA", [128, 18176], F32)
        PS = Buf("PS", es.enter_context(nc.psum_tensor("PS", [128, 4096], F32)))
        PB = [Buf("PB%d" % i, None) for i in range(8)]

        def pbank(i, n=512, p=128, j=0):
            return PS.t[0:p, i * 512 + j:i * 512 + j + n]

        def carve(plan):
            out = {}
            off = 0
            for name, ncol, dt, shape in plan:
                ap = ARENA.t[:, off:off + ncol]
                if dt is BF16:
                    ap = ap.bitcast(BF16)
                if len(shape) == 3:
                    ap = ap.rearrange("p (a b) -> p a b", b=shape[2])
                b = Buf(name, ap)
                out[name] = b
                off += ncol
            assert off <= 18176, off
            return out

        def cast_dma(buf, dst, src):
            n = dst.shape[-1]
            if n > 512 and len(dst.shape) == 2:
                dst = dst.rearrange("p (a b) -> p a b", b=512)
                src = src.rearrange("p (a b) -> p a b", b=512)
            S.dma("pool", dst, src, buf, True)

        def P(fn, reads=(), writes=()):
            S.op("pool", fn, reads, writes)

        def V(fn, reads=(), writes=()):
            S.op("dve", fn, reads, writes)

        def A(fn, reads=(), writes=()):
            S.op("act", fn, reads, writes)

        def T(fn, reads=(), writes=(), inc=True):
            S.op("pe", fn, reads, writes, inc)

        P(lambda g: g.memset(identF[:], 0.0), (), (identF,))
        P(lambda g: g.affine_select(out=identF[:], in_=identF[:], pattern=[[-1, 128]], compare_op=ALU.not_equal,
                                    fill=1.0, base=0, channel_multiplier=1), (identF,), (identF,))
        P(lambda g: g.tensor_copy(out=identB[:], in_=identF[:]), (identF,), (identB,))
        P(lambda g: g.memset(onesB[:], 1.0), (), (onesB,))
        P(lambda g: g.memset(maskP[:], 1.0), (), (maskP,))
        P(lambda g: g.affine_select(out=maskP[:], in_=maskP[:], pattern=[[1, 128]], compare_op=ALU.is_ge,
                                    fill=0.0, base=0, channel_multiplier=-1), (maskP,), (maskP,))
        P(lambda g: g.tensor_copy(out=maskU[:], in_=maskP[:]), (maskP,), (maskU,))
        P(lambda g: g.memset(maskP[0:64, 64:128], 0.0), (maskP,), (maskP,))
        P(lambda g: g.memset(maskS[:], 1.0), (), (maskS,))
        mS3 = maskS[:].rearrange("p (a b) -> p a b", b=4)
        P(lambda g: g.affine_select(out=mS3, in_=mS3, pattern=[[4, 16], [1, 4]], compare_op=ALU.is_ge,
                                    fill=0.0, base=0, channel_multiplier=-1), (maskS,), (maskS,))
        P(lambda g: g.affine_select(out=mS3, in_=mS3, pattern=[[-4, 16], [0, 4]], compare_op=ALU.is_ge,
                                    fill=0.0, base=0, channel_multiplier=1), (maskS,), (maskS,))
        P(lambda g: g.memset(rowsel[:], 1.0), (), (rowsel,))
        P(lambda g: g.affine_select(out=rowsel[:], in_=rowsel[:], pattern=[[-4, 16]], compare_op=ALU.is_ge,
                                    fill=0.0, base=0, channel_multiplier=1), (rowsel,), (rowsel,))
        P(lambda g: g.affine_select(out=rowsel[:], in_=rowsel[:], pattern=[[4, 16]], compare_op=ALU.is_ge,
                                    fill=0.0, base=3, channel_multiplier=-1), (rowsel,), (rowsel,))
        P(lambda g: g.memset(rm[:], 1.0), (), (rm,))
        P(lambda g: g.memset(rm[:].rearrange("p (c t) -> p c t", t=64)[:, :, 0:1], 0.0), (rm,), (rm,))
        P(lambda g: g.memset(rmS[:], 1.0), (), (rmS,))
        P(lambda g: g.memset(rmS[:].rearrange("p (c t) -> p c t", t=4)[:, :, 0:1], 0.0), (rmS,), (rmS,))

        S.dma("sp", lbc[:, 0:16], lbl[0], lbc, True)
        S.dma("sp", lbc[:, 16:32], lbl[1], lbc, True)
        S.dma("sp", gnc[:], gnm[:, :], gnc, True)
        cast_dma(WST, WST[:], wst[:, :])
        cast_dma(WSM, WSM[:], wsm[:, :])
        V(lambda v: v.tensor_tensor(out=lbc[:, 32:48], in0=lbc[:, 0:16], in1=lbc[:, 16:32], op=ALU.subtract), (lbc,), (lbc,))
        A(lambda a: a.activation(out=lbc[:, 32:48], in_=lbc[:, 32:48], func=AF.Tanh, scale=0.5), (lbc,), (lbc,))
        V(lambda v: v.tensor_scalar(out=lbc[:, 0:16], in0=lbc[:, 32:48], scalar1=0.25, scalar2=0.75, op0=ALU.mult, op1=ALU.add), (lbc,), (lbc,))
        V(lambda v: v.tensor_scalar(out=lbc[:, 16:32], in0=lbc[:, 32:48], scalar1=-0.25, scalar2=0.25, op0=ALU.mult, op1=ALU.add), (lbc,), (lbc,))
        V(lambda v: v.tensor_scalar(out=lbc[:, 48:64], in0=lbc[:, 32:48], scalar1=0.25, scalar2=-0.25, op0=ALU.mult, op1=ALU.add), (lbc,), (lbc,))
        V(lambda v: v.memset(lbc[:, 64:65], EPS), (lbc,), (lbc,))
        V(lambda v: v.memset(lbc[:, 65:66], LN_HALF), (lbc,), (lbc,))
        V(lambda v: v.memset(lbc[:, 66:67], 0.0), (lbc,), (lbc,))
        V(lambda v: v.memset(lbc[:, 67:68], 4.0 * EPS), (lbc,), (lbc,))
        epsc = lbc[:, 64:65]
        lnhc = lbc[:, 65:66]
        W3 = WST[:].rearrange("p (g t) -> p g t", t=128)
        mP3 = maskP[:]
        for g in range(16):
            P(lambda e, g=g: e.tensor_tensor(out=W3[:, g, :], in0=W3[:, g, :], in1=maskU[:], op=ALU.mult), (WST, maskU), (WST,))
        Wm3 = WSM[:].rearrange("p (g t) -> p g t", t=64)
        for g in range(16):
            P(lambda e, g=g: e.tensor_tensor(out=Wm3[:, g, :], in0=Wm3[:, g, :], in1=maskS[:], op=ALU.mult), (WSM, maskS), (WSM,))
        for h in range(H):
            P(lambda e, h=h: e.memset(SC[h][:], 0.0), (), (SC[h],))

        def x_issue(j, stg):
            rows = 128 if j < 8 else 64
            src = xp[1024 + j * 128:1024 + j * 128 + rows, :] if j < 8 else xs[0:64, :]
            S.dma("sp", stg[0:rows, 0:1024], src, stg, True)

        def x_transpose(j, stg, banks):
            rows = 128 if j < 8 else 64
            bi = j // 4 if j < 8 else 2
            c0_ = (j % 4) * 128 if j < 8 else 0
            for half in range(2):
                bk = banks[half]
                for kc4 in range(4):
                    kc = half * 4 + kc4
                    T(lambda t, kc=kc, kc4=kc4, bk=bk: t.transpose(
                        pbank(bk, rows, 128, kc4 * 128), stg[0:rows, kc * 128:(kc + 1) * 128], identF[0:rows, 0:rows]),
                      (stg, identF), (PB[bk],), inc=(kc4 == 3))
                src = pbank(bk, 512).rearrange("p (a b) -> p a b", b=128)[:, :, 0:rows]
                dst = XT[bi][:, half * 4:half * 4 + 4, c0_:c0_ + rows]
                if half == 0:
                    A(lambda a, src=src, dst=dst: a.activation(out=dst, in_=src, func=AF.Copy), (PB[bk],), (XT[bi],))
                else:
                    V(lambda v, src=src, dst=dst: v.tensor_copy(out=dst, in_=src), (PB[bk],), (XT[bi],))

        def load_x_block(sbi, bi, ntile_rows, xsrc, row0):
            nt = 4 if ntile_rows == 128 else 1
            for ti in range(nt):
                xin = XIN[ti % 2]
                rows = ntile_rows
                S.dma("sp", xin[0:rows, :], xsrc[row0 + ti * 128:row0 + ti * 128 + rows, :], xin, True)
                for half in range(2):
                    bk = 6 + half
                    for kc4 in range(4):
                        kc = half * 4 + kc4
                        T(lambda t, kc=kc, kc4=kc4, bk=bk, xin=xin, rows=rows: t.transpose(
                            pbank(bk, rows, 128, kc4 * 128), xin[0:rows, kc * 128:(kc + 1) * 128], identF[0:rows, 0:rows]),
                          (xin, identF), (PB[bk],), inc=(kc4 == 3))
                    src = pbank(bk, 512).rearrange("p (a b) -> p a b", b=128)[:, :, 0:rows]
                    dst = XT[bi][:, half * 4:half * 4 + 4, ti * 128:ti * 128 + rows]
                    if half == 0:
                        A(lambda a, src=src, dst=dst: a.activation(out=dst, in_=src, func=AF.Copy), (PB[bk],), (XT[bi],))
                    else:
                        V(lambda v, src=src, dst=dst: v.tensor_copy(out=dst, in_=src), (PB[bk],), (XT[bi],))

        def l0_unit(h, bi, W, wk, st):
            samp = bi == 2
            N = 64 if samp else 512
            W3h = W[:].rearrange("p (k c) -> p k c", c=512)
            xt = XT[bi]
            c0 = lbc[:, h:h + 1]
            c1 = lbc[:, 16 + h:17 + h]
            nc1 = lbc[:, 48 + h:49 + h]
            pz, pq, pg, pv = 0, 1, 2, 0

            def proj(bk, sl):
                for kc in range(8):
                    T(lambda t, kc=kc: MM(t, pbank(bk, N), lhsT=W3h[:, kc, sl * 128:(sl + 1) * 128], rhs=xt[:, kc, :],
                                               start=(kc == 0), stop=(kc == 7)),
                      (W, xt), (PB[bk],), inc=(kc == 7))
            tz, tq, tg, kn, cum, lf = wk["tz"], wk["tq"], wk["tg"], wk["kn"], wk["cum"], wk["lf"]
            ntile = 1 if samp else 4
            rows = 64 if samp else 128
            csz = 4 if samp else 64
            nch = N // csz
            proj(pz, 1)
            proj(pq, 0)
            proj(pg, 3)
            A(lambda a: a.activation(out=tz[:, 0:N], in_=pbank(pz, N), func=AF.Tanh, scale=0.5), (PB[pz],), (tz,))
            A(lambda a: a.activation(out=tq[:, 0:N], in_=pbank(pq, N), func=AF.Silu), (PB[pq],), (tq,))
            A(lambda a: a.activation(out=tg[:, 0:N], in_=pbank(pg, N), func=AF.Silu), (PB[pg],), (tg,))
            P(lambda e: e.tensor_scalar(out=kn[:, 0:N], in0=tz[:, 0:N], scalar1=nc1, scalar2=c1, op0=ALU.mult, op1=ALU.add),
              (tz, lbc), (kn,))
            A(lambda a: a.activation(out=lf[:, 0:N], in_=tz[:, 0:N], func=AF.Ln, scale=c1, bias=c0), (tz, lbc), (lf,))
            if samp:
                S0h = [st["S0a"], st["S0c"]]
            yield
            for ti in range(ntile):
                for kc in range(8):
                    T(lambda t, kc=kc, ti=ti: MM(t, pbank(pv, 128, rows, ti * 128), lhsT=xt[:, kc, ti * 128:ti * 128 + rows],
                                                     rhs=W3h[:, kc, 256:384], start=(ti == 0 and kc == 0), stop=(kc == 7)),
                      (W, xt), (PB[pv],), inc=(ti == ntile - 1 and kc == 7))
            rmask = rmS if samp else rm
            V(lambda v: v.tensor_tensor_scan(out=cum[:, 0:N], data0=rmask[:, 0:N], data1=lf[:, 0:N], initial=0.0,
                                             op0=ALU.mult, op1=ALU.add), (rmask, lf), (cum,))
            Vt = wk["V"]
            A(lambda a: a.activation(out=Vt[0:rows, 0:ntile * 128], in_=pbank(pv, ntile * 128, rows), func=AF.Copy),
              (PB[pv],), (Vt,))
            E1 = wk["E1"]
            A(lambda a: a.activation(out=E1[:, 0:N], in_=cum[:, 0:N], func=AF.Exp), (cum,), (E1,))
            A(lambda a: a.activation(out=lf[:, 0:N], in_=cum[:, 0:N], func=AF.Exp, scale=-1.0), (cum,), (lf,))
            gl = wk["gl"]
            clv = cum[:, 0:N].rearrange("p (c t) -> p c t", t=csz)[:, :, csz - 1:csz]
            A(lambda a: a.activation(out=gl[:, 0:nch].rearrange("p (c o) -> p c o", o=1), in_=clv, func=AF.Exp), (cum,), (gl,))
            Ab = wk["A"]
            P(lambda v: v.tensor_tensor(out=Ab[:, 0:N], in0=tq[:, 0:N], in1=E1[:, 0:N], op=ALU.mult), (tq, E1), (Ab,))
            Bb = wk["B"]
            P(lambda e: e.tensor_tensor(out=Bb[:, 0:N], in0=kn[:, 0:N], in1=lf[:, 0:N], op=ALU.mult), (kn, lf), (Bb,))
            KT = wk["KT"]
            P(lambda e: e.tensor_tensor(out=KT[:, 0:N].rearrange("p (c t) -> p c t", t=csz),
                                        in0=Bb[:, 0:N].rearrange("p (c t) -> p c t", t=csz),
                                        in1=gl[:, 0:nch].rearrange("p (c o) -> p c o", o=1).to_broadcast([128, nch, csz]),
                                        op=ALU.mult), (Bb, gl), (KT,))
            yield
            pkt, psc, po, pu0, pu1 = 3, 4, 5, 6, 7
            for ti in range(ntile):
                T(lambda t, ti=ti: MM(t, pbank(pkt, 128, rows, ti * 128), lhsT=KT[:, ti * 128:ti * 128 + rows], rhs=identB[:],
                                            start=(ti == 0), stop=True), (KT, identB), (PB[pkt],), inc=(ti == ntile - 1))
            Ke = st["Ke"]
            A(lambda a: a.activation(out=Ke[0:rows, 0:ntile * 128], in_=pbank(pkt, ntile * 128, rows), func=AF.Copy),
              (PB[pkt],), (Ke,))
            for ti in range(ntile):
                T(lambda t, ti=ti: MM(t, pbank(psc, rows, rows, ti * 128), lhsT=Bb[:, ti * 128:ti * 128 + rows],
                                            rhs=Ab[:, ti * 128:ti * 128 + rows], start=(ti == 0), stop=True),
                  (Ab, Bb), (PB[psc],), inc=(ti == ntile - 1))
            Sc = st["Sc"]
            if samp:
                V(lambda v: v.tensor_tensor(out=Sc[0:64, 0:64], in0=pbank(psc, 64, 64), in1=maskS[:], op=ALU.mult),
                  (PB[psc], maskS), (Sc,))
            else:
                V(lambda v: v.tensor_tensor(out=Sc[:, 0:512].rearrange("p (a b) -> p a b", b=128),
                                            in0=pbank(psc, 512).rearrange("p (a b) -> p a b", b=128),
                                            in1=maskP[:].rearrange("p (o b) -> p o b", o=1).to_broadcast([128, 4, 128]),
                                            op=ALU.mult), (PB[psc], maskP), (Sc,))
            if samp:
                S0b = st["S0b"]
                VM = st["VM"]
                V(lambda e: e.tensor_tensor(
                    out=VM[0:64, :].rearrange("p (i v) -> p i v", v=128),
                    in0=Vt[0:64, 0:128].rearrange("p (o v) -> p o v", o=1).to_broadcast([64, 16, 128]),
                    in1=rowsel[:, 0:16].rearrange("p (i o) -> p i o", o=1).to_broadcast([64, 16, 128]),
                    op=ALU.mult), (Vt, rowsel), (VM,))
                for hf in range(2):
                    S0 = S0h[hf]
                    A(lambda a, hf=hf: a.activation(out=S0b[:, hf * 1024:(hf + 1) * 1024], in_=S0[:, :], func=AF.Copy), (S0,), (S0b,))
                    P(lambda e, hf=hf: e.tensor_tensor(
                        out=S0[:, :].rearrange("p (i v) -> p i v", v=128), in0=S0[:, :].rearrange("p (i v) -> p i v", v=128),
                        in1=gl[:, hf * 8:hf * 8 + 8].rearrange("p (i o) -> p i o", o=1).to_broadcast([128, 8, 128]),
                        op=ALU.mult), (S0, gl), (S0,))
            yield
            if not samp:
                for n in range(8):
                    ti, hf = n // 2, n % 2
                    bk = pu0 + hf
                    T(lambda t, n=n, ti=ti, hf=hf, bk=bk: MM(t,
                        pbank(bk, 128, 128, ti * 128), lhsT=Ke[hf * 64:hf * 64 + 64, ti * 128:ti * 128 + 128],
                        rhs=Vt[hf * 64:hf * 64 + 64, ti * 128:ti * 128 + 128], start=(ti == 0), stop=True),
                      (Ke, Vt), (PB[bk],), inc=(ti == 3))
                Sall = st["Sall"]
                if bi == 0:
                    V(lambda v: v.tensor_copy(out=Sall[:, 0, :], in_=SC[h][:]), (SC[h],), (Sall,))
                else:
                    V(lambda v: v.tensor_copy(out=Sall[:, 0, :], in_=Sall[:, 8, :]), (Sall,), (Sall,))
                for n in range(8):
                    bk = pu0 + (n % 2)
                    V(lambda v, n=n, bk=bk: v.scalar_tensor_tensor(out=Sall[:, n + 1, :], in0=Sall[:, n, :], scalar=gl[:, n:n + 1],
                                                                in1=pbank(bk, 128, 128, (n // 2) * 128), op0=ALU.mult, op1=ALU.add),
                      (Sall, gl, PB[bk]), (Sall,))
                Sbf = st["Sbf"]
                V(lambda v: v.tensor_copy(out=Sbf[:, 0:8, :], in_=Sall[:, 0:8, :]), (Sall,), (Sbf,))
                yield
                if bi == 1:
                    P(lambda e: e.tensor_copy(out=SC[h][:], in_=Sall[:, 8, :]), (Sall,), (SC[h],))
                for ti in range(4):
                    T(lambda t, ti=ti: MM(t, pbank(po, 128, 128, ti * 128), lhsT=Vt[:, ti * 128:ti * 128 + 128],
                                                rhs=Sc[:, ti * 128:ti * 128 + 128], start=(ti == 0), stop=False),
                      (Vt, Sc), (PB[po],), inc=False)
                for n in range(8):
                    T(lambda t, n=n: MM(t, pbank(po, 64, 128, n * 64), lhsT=Sbf[:, n, :], rhs=Ab[:, n * 64:n * 64 + 64],
                                              start=False, stop=True), (Sbf, Ab), (PB[po],), inc=(n == 7))
            else:
                S0b = st["S0b"]
                VM = st["VM"]
                for hf in range(2):
                    S0 = S0h[hf]
                    for j in range(2):
                        bk = pu0 + j
                        T(lambda t, j=j, bk=bk, hf=hf: MM(t, pbank(bk, 512), lhsT=Ke[0:64, 0:128],
                                                        rhs=VM[0:64, hf * 1024 + j * 512:hf * 1024 + (j + 1) * 512],
                                                        start=True, stop=True), (Ke, VM), (PB[bk],), inc=True)
                    V(lambda v: v.tensor_tensor(out=S0[:, :], in0=S0[:, :], in1=PS.t[:, pu0 * 512:pu0 * 512 + 1024], op=ALU.add),
                      (S0, PB[pu0], PB[pu1]), (S0,))
                    S.dma("sp", sso[h, :, hf * 1024:(hf + 1) * 1024], S0[:, :], S0, False, is_out=True)
                yield
                T(lambda t: MM(t, pbank(po, 64), lhsT=Vt[0:64, 0:128], rhs=Sc[0:64, 0:64], start=True, stop=False),
                  (Vt, Sc), (PB[po],), inc=False)
                for i in range(16):
                    T(lambda t, i=i: MM(t, pbank(po, 4, 128, i * 4), lhsT=S0b[:, i * 128:(i + 1) * 128], rhs=Ab[:, i * 4:i * 4 + 4],
                                              start=False, stop=True), (S0b, Ab), (PB[po],), inc=(i == 15))
            osq = st["osq"]
            A(lambda a: a.activation(out=osq[:, 0:N], in_=pbank(po, N), func=AF.Square), (PB[po],), (osq,))
            yield
            T(lambda t: MM(t, pbank(psc, N), lhsT=onesB[:], rhs=osq[:, 0:N], start=True, stop=True), (onesB, osq), (PB[psc],))
            rs = st["rs"]
            A(lambda a: a.activation(out=rs[:, 0:N], in_=pbank(psc, N), func=AF.Ln, scale=1.0 / 128.0, bias=epsc), (PB[psc], lbc), (rs,))
            A(lambda a: a.activation(out=rs[:, 0:N], in_=rs[:, 0:N], func=AF.Exp, scale=-0.5), (rs,), (rs,))
            t1 = st["t1"]
            V(lambda v: v.scalar_tensor_tensor(out=t1[:, 0:N], in0=pbank(po, N), scalar=gnc[:, h:h + 1], in1=rs[:, 0:N],
                                               op0=ALU.mult, op1=ALU.mult), (PB[po], gnc, rs), (t1,))
            ot = OT[h][bi]
            P(lambda v: v.tensor_tensor(out=ot[:, 0:N], in0=t1[:, 0:N], in1=tg[:, 0:N], op=ALU.mult), (t1, tg), (ot,))

        def ln_tile(*a, **k):
            for _ in ln_gen(*a, **k):
                pass

        def ln_gen(y, rows, ncol, stt, gam=None, bet=None, eng_gb="pool", epsap=None):
            yb, yap = y
            nchk = ncol // 512
            V(lambda v: [v.bn_stats(out=stt["bst"][0:rows, c * 6:(c + 1) * 6], in_=yap[:, c * 512:(c + 1) * 512]) for c in range(nchk)][-1],
              (yb,), (stt["bst"],))
            V(lambda v: v.bn_aggr(out=stt["mv"][0:rows, 0:2], in_=stt["bst"][0:rows, 0:nchk * 6]),
              (stt["bst"],), (stt["mv"],))
            mv = stt["mv"]
            yield
            A(lambda a: a.activation(out=mv[0:rows, 2:3], in_=mv[0:rows, 1:2], func=AF.Ln, scale=1.0, bias=(epsc if epsap is None else epsap)[0:rows, :]), (mv, lbc), (mv,))
            A(lambda a: a.activation(out=mv[0:rows, 2:3], in_=mv[0:rows, 2:3], func=AF.Exp, scale=-0.5), (mv,), (mv,))
            yield
            V(lambda v: v.scalar_tensor_tensor(out=mv[0:rows, 3:4], in0=mv[0:rows, 0:1], scalar=-1.0, in1=mv[0:rows, 2:3],
                                               op0=ALU.mult, op1=ALU.mult), (mv,), (mv,))
            A(lambda a: a.activation(out=yap, in_=yap, func=AF.Identity, scale=mv[0:rows, 2:3], bias=mv[0:rows, 3:4]), (yb, mv), (yb,))
            if gam is not None:
                S.op(eng_gb, lambda e: e.tensor_tensor(out=yap, in0=yap, in1=gam[0:rows, 0:ncol], op=ALU.mult), (yb, gam), (yb,))
                S.op(eng_gb, lambda e: e.tensor_tensor(out=yap, in0=yap, in1=bet[0:rows, 0:ncol], op=ALU.add), (yb, bet), (yb,))

        for sbi in range(2):
            nblk = 2 if sbi == 0 else 3
            plan = []
            for i in range(2):
                for nm in ("tz", "tq", "kn", "cum", "E1", "lf"):
                    plan.append(("%s%d" % (nm, i), 512, F32, [128, 512]))
                for nm in ("A", "B", "KT", "V"):
                    plan.append(("%s%d" % (nm, i), 256, BF16, [128, 512]))
                plan.append(("gl%d" % i, 16, F32, [128, 16]))
            plan += [("tgA", 512, F32, [128, 512]), ("tgB", 512, F32, [128, 512]), ("tgC", 512, F32, [128, 512]),
                     ("Ke", 256, BF16, [128, 512]), ("Sc", 256, BF16, [128, 512]), ("ScB", 256, BF16, [128, 512]),
                     ("Sall", 9 * 128, F32, [128, 9, 128]), ("Sbf", 512, BF16, [128, 8, 128]),
                     ("osq", 256, BF16, [128, 512]), ("rs", 512, F32, [128, 512]), ("t1", 512, F32, [128, 512]),
                     ("S0a", 1024, F32, [128, 1024]), ("S0c", 1024, F32, [128, 1024]), ("S0b", 1024, BF16, [128, 2048]), ("VM", 1024, BF16, [128, 2048])]
            ar = carve(plan)
            wks = [{k[:-1]: v for k, v in ar.items() if k.endswith(str(i)) and k[:-1] in ("tz", "tq", "kn", "cum", "E1", "lf", "A", "B", "KT", "V", "gl")}
                   for i in range(2)]
            st = ar
            if sbi == 0:
                load_x_block(sbi, 0, 128, xp, 0)
            units = [(h, bi) for h in range(H) for bi in range(2)]
            gens = {}
            tgs = [ar["tgA"], ar["tgB"], ar["tgC"]]

            def step(g):
                if g is not None:
                    next(g, None)

            if sbi == 0:
                cast_dma(WH[0], WH[0][:], wia[0])
                cast_dma(WH[1], WH[1][:], wia[1])
            S.dma("sp", LNG[:], lng[0, 0, :].partition_broadcast(128), LNG, True)
            S.dma("sp", LNB[:], lnb[0, 0, :].partition_broadcast(128), LNB, True)
            for i in range(len(units) + 2):
                if i < len(units):
                    h, bi = units[i]
                    if bi == 0 and 1 <= h and h + 1 < H:
                        cast_dma(WH[(h + 1) % 2], WH[(h + 1) % 2][:], wia[h + 1])
                    if nblk == 3 and bi == 0 and h == H - 1:
                        cast_dma(WH[0], WH[0][:], wia[0])
                    if 2 <= i < 10:
                        j = i - 2
                        cast_dma(WBIG, WBIG[:, j * 2048:(j + 1) * 2048], woa[:, j * 2048:(j + 1) * 2048])
                    wk = dict(wks[i % 2])
                    wk["tg"] = tgs[i % 3]
                    st_i = dict(st)
                    st_i["Sc"] = st["Sc"] if i % 2 == 0 else st["ScB"]
                    gens[i] = l0_unit(h, bi, WH[h % 2], wk, st_i)
                    step(gens[i])
                    if sbi == 0 and i == 0:
                        load_x_block(sbi, 1, 128, xp, 512)
                step(gens.get(i - 1))
                step(gens.get(i - 2))
                if i - 2 in gens:
                    hh, bb = units[i - 2]
                    if sbi == 1 and bb == 1:
                        S.dma("sp", spo[hh], SC[hh][:], SC[hh], False, is_out=True)
                step(gens.get(i))
                step(gens.get(i - 1))
                step(gens.get(i - 2))
                if nblk == 3 and i == len(units) - 1:
                    cast_dma(WH[1], WH[1][:], wia[1])
            S.barrier()
            if nblk == 3:
                plan = []
                for k in range(3):
                    for nm in ("tz", "tq", "kn", "cum", "E1", "lf", "tg", "rs", "t1"):
                        plan.append(("%s%d" % (nm, k), 64, F32, [128, 64]))
                    for nm in ("A", "B", "KT", "Sc", "osq"):
                        plan.append(("%s%d" % (nm, k), 32, BF16, [128, 64]))
                    plan += [("V%d" % k, 64, BF16, [128, 128]), ("Ke%d" % k, 64, BF16, [128, 128]), ("gl%d" % k, 16, F32, [128, 16]),
                             ("S0a%d" % k, 1024, F32, [128, 1024]), ("S0c%d" % k, 1024, F32, [128, 1024]),
                             ("S0b%d" % k, 1024, BF16, [128, 2048]), ("VM%d" % k, 1024, BF16, [128, 2048])]
                ars = carve(plan)
                sets = [{k_[:-1]: v for k_, v in ars.items() if k_.endswith(str(k))} for k in range(3)]
                gens = {}
                for i in range(H + 2):
                    if i < H:
                        h = i
                        if 1 <= h and h + 1 < H:
                            cast_dma(WH[(h + 1) % 2], WH[(h + 1) % 2][:], wia[h + 1])
                        sd = sets[i % 3]
                        for hf in range(2):
                            S0x = sd["S0a"] if hf == 0 else sd["S0c"]
                            S.dma("sp", S0x[:, :], s0[h, :, hf * 1024:(hf + 1) * 1024], S0x, True)
                        gens[i] = l0_unit(h, 2, WH[h % 2], sd, sd)
                        step(gens[i])
                    step(gens.get(i - 1))
                    step(gens.get(i - 2))
                    step(gens.get(i))
                    step(gens.get(i - 1))
                    step(gens.get(i - 2))
                S.barrier()

            ntl = 8 if sbi == 0 else 9
            plan = [("X1", 9 * 1024, F32, [128, 9, 1024]),
                    ("VG", 2048, F32, [128, 2048]), ("NG", 1024, BF16, [128, 2048]), ("GB", 1024, BF16, [128, 2048]),
                    ("VGb", 2048, F32, [128, 2048]),
                    ("bst", 32, F32, [128, 32]), ("mv", 8, F32, [128, 8]),
                    ("RR", 1024, BF16, [128, 2048]), ("RRS", 512, BF16, [128, 1024]), ("LLg", 1024, BF16, [128, 2048])]
            ar = carve(plan)
            X1 = [Buf("X1_%d" % i, ar["X1"].t[:, i, :]) for i in range(9)]
            Tsets = [[Alias(base, base.t[:, j * 512:(j + 1) * 512]) for j in range(3)]
                     for k, base in enumerate((ar["VGb"], ar["VG"]))]
            RR, RRS, LLg = ar["RR"], ar["RRS"], ar["LLg"]
            for g in range(2):
                cast_dma(WH[g], WH[g][:, 0:2048], wuz[g])

            def tile_info(ti):
                if ti < 8:
                    return 128, ti // 4, (ti % 4) * 128
                return 64, 2, 0

            def outproj_ln(ti, src_bufs, resid, out_ap, l, split=False):
                rows, bi, c0_ = tile_info(ti)
                WB3 = WBIG[:].rearrange("p (e f) -> p e f", f=1024)
                WS1 = WBIG[:, 0:8192].rearrange("p (e f) -> p e f", f=512)
                WS0 = [WH[k][:, 0:4096].rearrange("p (e f) -> p e f", f=512) for k in range(2)]
                db = (ti % 2) * 2
                for nh in range(2):
                    for e in range(16):
                        if not split:
                            rhs, wb = WB3[:, e, nh * 512:(nh + 1) * 512], WBIG
                        elif nh == 0:
                            rhs, wb = WS0[e // 8][:, e % 8, :], WH[e // 8]
                        else:
                            rhs, wb = WS1[:, e, :], WBIG
                        T(lambda t, e=e, nh=nh, rhs=rhs: MM(t, pbank(db + nh, 512, rows), lhsT=src_bufs[e][bi][:, c0_:c0_ + rows],
                                                       rhs=rhs, start=(e == 0), stop=(e == 15)),
                          (src_bufs[e][bi], wb), (PB[db + nh],), inc=(e == 15))
                rb, rap = resid
                ob, oap = out_ap
                V(lambda v: v.scalar_tensor_tensor(out=oap, in0=rap, scalar=ALPHA, in1=PS.t[0:rows, db * 512:db * 512 + 1024], op0=ALU.mult, op1=ALU.add),
                  (rb, PB[db], PB[db + 1]), (ob,))
                ln_tile((ob, oap), rows, 1024, ar, LNG, LNB)

            def tail_transposes(ti):
                rows, bi, c0_ = tile_info(ti)
                for half in range(2):
                    bk = 4 + half + 2 * (ti % 2)
                    for kc4 in range(4):
                        kc = half * 4 + kc4
                        T(lambda t, kc=kc, kc4=kc4, bk=bk: t.transpose(
                            pbank(bk, rows, 128, kc4 * 128), X1[ti][0:rows, kc * 128:(kc + 1) * 128], identF[0:rows, 0:rows]),
                          (X1[ti], identF), (PB[bk],), inc=(kc4 == 3))
                    src = pbank(bk, 512).rearrange("p (a b) -> p a b", b=128)[:, :, 0:rows]
                    dst = XT[bi][:, half * 4:half * 4 + 4, c0_:c0_ + rows]
                    if half == 0:
                        A(lambda a, src=src, dst=dst: a.activation(out=dst, in_=src, func=AF.Copy), (PB[bk],), (XT[bi],))
                    else:
                        V(lambda v, src=src, dst=dst: v.tensor_copy(out=dst, in_=src), (PB[bk],), (XT[bi],))

            for ti in range(ntl):
                rows, bi, c0_ = tile_info(ti)
                xin = XIN[ti % 2]
                if ti < 8:
                    S.dma("sp", xin[0:rows, :], xp[sbi * 1024 + ti * 128:sbi * 1024 + ti * 128 + rows, :], xin, True)
                else:
                    S.dma("sp", xin[0:rows, :], xs[0:rows, :], xin, True)
                outproj_ln(ti, OT, (xin, xin[0:rows, :]), (X1[ti], X1[ti][0:rows, :]), 0)
                if ti > 0:
                    tail_transposes(ti - 1)
            tail_transposes(ntl - 1)

            GB = ar["GB"]
            cast_dma(GB, GB[:], lvg[0, :].partition_broadcast(128))
            V(lambda e: e.tensor_scalar(out=GB[:], in0=GB[:], scalar1=0.5, scalar2=None, op0=ALU.mult), (GB,), (GB,))
            S.dma("sp", LNG[:], lng[1, 0, :].partition_broadcast(128), LNG, True)
            S.dma("sp", LNB[:], lnb[1, 0, :].partition_broadcast(128), LNB, True)
            for j in range(4):
                T(lambda t, j=j: MM(t, pbank(4 + j, 512, 2), lhsT=onesB[:, 0:2], rhs=WST[:, j * 512:(j + 1) * 512], start=True, stop=True),
                  (onesB, WST), (PB[4 + j],))
            A(lambda a: a.activation(out=RR[0:1, :], in_=PS.t[0:1, 2048:4096], func=AF.Copy), (PB[4], PB[5], PB[6], PB[7]), (RR,))
            cast_dma(RR, RR[1:2, :], bs[0:1, :])
            for j in range(2):
                T(lambda t, j=j: MM(t, pbank(4 + j, 512, 2), lhsT=onesB[0:64, 0:2], rhs=WSM[:, j * 512:(j + 1) * 512], start=True, stop=True),
                  (onesB, WSM), (PB[4 + j],))
            A(lambda a: a.activation(out=RRS[0:1, :], in_=PS.t[0:1, 2048:3072], func=AF.Copy), (PB[4], PB[5]), (RRS,))
            cast_dma(RRS, RRS[1:2, :], bss[0:1, :])
            P(lambda e: e.memset(LLg[0:2, :], 1.0), (), (LLg,))
            cast_dma(LLg, LLg[0:1, :], lvbc[0:1, :])
            V(lambda e: e.tensor_scalar(out=LLg[0:2, :], in0=LLg[0:2, :], scalar1=0.5, scalar2=None, op0=ALU.mult), (LLg,), (LLg,))
            it = 0
            for g in range(16):
                W = WH[g % 2]
                Wg = W[:, 0:2048].rearrange("p (k c) -> p k c", c=256)
                for bi in range(nblk):
                    N = 512 if bi < 2 else 64
                    xt = XT[bi]
                    pu_, pz_ = (it % 2) * 2, (it % 2) * 2 + 1
                    T1, T2, T3 = Tsets[it % 2]
                    it += 1
                    for sl, bk in ((0, pu_), (1, pz_)):
                        for kc in range(8):
                            T(lambda t, kc=kc, sl=sl, bk=bk: MM(t, pbank(bk, N), lhsT=Wg[:, kc, sl * 128:(sl + 1) * 128], rhs=xt[:, kc, :],
                                                                   start=(kc == 0), stop=(kc == 7)), (W, xt), (PB[bk],), inc=(kc == 7))
                    A(lambda a: a.activation(out=T2[:, 0:N], in_=pbank(pu_, N), func=AF.Gelu_apprx_tanh), (PB[pu_],), (T2,))
                    A(lambda a: a.activation(out=T3[:, 0:N], in_=pbank(pz_, N), func=AF.Tanh, scale=0.5), (PB[pz_],), (T3,))
                    V(lambda v: v.scalar_tensor_tensor(out=T3[:, 0:N], in0=T3[:, 0:N], scalar=1.0, in1=pbank(pz_, N), op0=ALU.add, op1=ALU.mult), (T3, PB[pz_]), (T3,))
                    ot = OT[g][bi]
                    P(lambda e: e.tensor_tensor(out=ot[:, 0:N], in0=T2[:, 0:N], in1=T3[:, 0:N], op=ALU.mult), (T2, T3), (ot,))
                if g + 2 < 16:
                    cast_dma(W, W[:, 0:2048], wuz[g + 2])
                if g < 8:
                    cast_dma(WBIG, WBIG[:, g * 2048:(g + 1) * 2048], wv[:, g * 2048:(g + 1) * 2048])
            WV3 = WBIG[:, 0:8 * 2048].rearrange("p (k c) -> p k c", c=2048)
            VGs = [ar["VG"], ar["VGb"]]
            NG = ar["NG"]
            TT = XIN[1]
            GBF = XIN[0]

            def v_proj(ti):
                rows, bi, c0_ = tile_info(ti)
                xt = XT[bi]
                VG = VGs[ti % 2]
                for nb in range(4):
                    bk = nb
                    TT = XIN[1 - (nb % 2)]
                    T4 = TT[0:rows, 0:512]
                    T5 = TT[0:rows, 512:1024]
                    for kc in range(8):
                        T(lambda t, kc=kc, nb=nb, bk=bk: MM(t, pbank(bk, 512, rows), lhsT=xt[:, kc, c0_:c0_ + rows],
                                                               rhs=WV3[:, kc, nb * 512:(nb + 1) * 512], start=(kc == 0), stop=(kc == 7)),
                          (WBIG, xt), (PB[bk],), inc=(kc == 7))
                    pv_ = pbank(bk, 512, rows)
                    vg = VG[0:rows, nb * 512:(nb + 1) * 512]
                    A(lambda a, pv_=pv_, vg=vg: a.activation(out=vg, in_=pv_, func=AF.Gelu_apprx_tanh), (PB[bk],), (VG,))
                    yield

            def v_ln(ti):
                rows, bi, c0_ = tile_info(ti)
                VG = VGs[ti % 2]
                for _ in ln_gen((VG, VG[0:rows, :]), rows, 2048, ar):
                    yield
                yield
                P(lambda e: e.tensor_tensor(out=NG[0:rows, :], in0=VG[0:rows, :], in1=GB[0:rows, :], op=ALU.mult), (VG, GB), (NG,))
                if ti == 8:
                    GF, BF = GBF[0:64, 0:512], GBF[0:64, 512:1024]
                    for nb in range(4):
                        S.dma("sp", GF, lvg[0, nb * 512:(nb + 1) * 512].partition_broadcast(64), GBF, True)
                        S.dma("sp", BF, lvb[0, nb * 512:(nb + 1) * 512].partition_broadcast(64), GBF, True)
                        P(lambda e, nb=nb: e.tensor_tensor(out=GF, in0=VG[0:64, nb * 512:(nb + 1) * 512], in1=GF, op=ALU.mult), (VG, GBF), (GBF,))
                        P(lambda e: e.tensor_tensor(out=GF, in0=GF, in1=BF, op=ALU.add), (GBF,), (GBF,))
                        S.dma("sp", cvo[:, nb * 512:(nb + 1) * 512], GF, GBF, False, is_out=True)

            def v_mix(ti):
                rows, bi, c0_ = tile_info(ti)
                tw = 128 if ti < 8 else 64
                Wsrc = WST if ti < 8 else WSM
                Rsrc = RR if ti < 8 else RRS
                for g in range(16):
                    bk = 4 + (g * tw) // 512
                    off = (g * tw) % 512
                    T(lambda t, g=g, bk=bk, off=off: MM(t,
                        pbank(bk, tw, 128, off), lhsT=NG[0:rows, g * 128:(g + 1) * 128], rhs=Wsrc[0:rows, g * tw:(g + 1) * tw],
                        start=(off == 0), stop=False), (NG, Wsrc), (PB[bk],), inc=False)
                for g in range(16):
                    bk = 4 + (g * tw) // 512
                    off = (g * tw) % 512
                    T(lambda t, g=g, bk=bk, off=off: MM(t,
                        pbank(bk, tw, 128, off), lhsT=LLg[0:2, g * 128:(g + 1) * 128], rhs=Rsrc[0:2, g * tw:(g + 1) * tw],
                        start=False, stop=True), (LLg, Rsrc), (PB[bk],), inc=(off + tw == 512 or g == 15))

            def v_mul(ti):
                rows, bi, c0_ = tile_info(ti)
                tw = 128 if ti < 8 else 64
                for g in range(16):
                    bk = 4 + (g * tw) // 512
                    off = (g * tw) % 512
                    ot = OT[g][bi]
                    V(lambda v, g=g, bk=bk, off=off, ot=ot: v.tensor_tensor(out=ot[:, c0_:c0_ + tw], in0=ot[:, c0_:c0_ + tw],
                                                                          in1=pbank(bk, tw, 128, off), op=ALU.mult), (ot, PB[bk]), (ot,))

            def run(g):
                for _ in g:
                    pass

            def nxt(g):
                if g is not None:
                    next(g, None)

            wob3 = wob[:, :].rearrange("p (e f) -> p e f", f=1024)
            if sbi == 1:
                for k in range(2):
                    S.dma("pool", WH[k][:, 0:4096].rearrange("p (e f) -> p e f", f=512), wob3[:, k * 8:(k + 1) * 8, 0:512], WH[k], True)
            run(v_proj(0))
            run(v_proj(1))
            run(v_ln(0))
            for ti in range(ntl):
                v_mix(ti)
                gl_ = v_ln(ti + 1) if ti + 1 < ntl else None
                gp_ = v_proj(ti + 2) if ti + 2 < ntl else None
                if gl_ is not None:
                    run(gl_)
                v_mul(ti)
                if gp_ is not None:
                    run(gp_)
                    if ti + 2 == ntl - 1:
                        if sbi == 0:
                            for j in range(4):
                                cast_dma(WBIG, WBIG[:, j * 4096:(j + 1) * 4096], wob[:, j * 4096:(j + 1) * 4096])
                        else:
                            for j in range(2):
                                S.dma("pool", WBIG[:, j * 4096:(j + 1) * 4096].rearrange("p (e f) -> p e f", f=512),
                                      wob3[:, j * 8:(j + 1) * 8, 512:1024], WBIG, True)
            stgs = [ar["VG"], ar["VGb"]]
            if sbi == 0:
                x_issue(0, stgs[0])
                x_issue(1, stgs[1])
                cast_dma(WH[0], WH[0][:], wia[0])
                cast_dma(WH[1], WH[1][:], wia[1])
            for ti in range(ntl):
                rows, bi, c0_ = tile_info(ti)
                yst = XIN[ti % 2]
                outproj_ln(ti, OT, (X1[ti], X1[ti][0:rows, :]), (yst, yst[0:rows, :]), 1, split=(sbi == 1))
                if sbi == 0:
                    x_transpose(ti, stgs[ti % 2], (4 + 2 * (ti % 2), 5 + 2 * (ti % 2)))
                    if ti + 2 < 9:
                        x_issue(ti + 2, stgs[ti % 2])
                if ti < 8:
                    S.dma("sp", yp[sbi * 1024 + ti * 128:sbi * 1024 + ti * 128 + rows, :], yst[0:rows, :], yst, False, is_out=True)
                else:
                    S.dma("sp", ys[0:rows, :], yst[0:rows, :], yst, False, is_out=True)
            if sbi == 0:
                x_transpose(8, stgs[0], (4, 5))
            S.barrier()
        S.finish()
        print("instructions:", S.ninst, "sems:", S.nsem)
    return nc


_NC_CACHE = {}


def kernel(x_prompt, x_sample, state_hgrn, w_in_a, lb_logits_a, gnorm_a, w_out_a, w_in_b, lnv_g_b, lnv_b_b,
           w_s_b, b_s_b, w_out_b, ln_g, ln_b):
    f = np.float32
    c = np.ascontiguousarray
    wia = c(np.asarray(w_in_a, f)[0].reshape(8, 128, 4, 16, 128).transpose(3, 1, 0, 2, 4).reshape(16, 128, 8 * 512))
    woa = c(np.asarray(w_out_a, f)[0].reshape(16, 128, 1024).transpose(1, 0, 2).reshape(128, 16 * 1024))
    wb = np.asarray(w_in_b, f)[0]
    wuz = c(np.stack([wb[:, 0:2048], wb[:, 4096:6144]], 0).reshape(2, 8, 128, 16, 128).transpose(3, 2, 1, 0, 4).reshape(16, 128, 8 * 256))
    wv = c(wb[:, 2048:4096].reshape(8, 128, 2048).transpose(1, 0, 2).reshape(128, 8 * 2048))
    wob = c(np.asarray(w_out_b, f)[0].reshape(16, 128, 1024).transpose(1, 0, 2).reshape(128, 16 * 1024))
    ws = np.asarray(w_s_b, f)[0]
    wst = c(ws.transpose(2, 0, 1).reshape(128, 16 * 128))
    w4 = ws[:, 0:4, 0:4].transpose(2, 0, 1)
    wsm = c(np.tile(w4[None, :, :, None, :], (16, 1, 1, 16, 1)).reshape(64, 16 * 64))
    bsr = np.asarray(b_s_b, f)[0]
    bs = c(bsr.reshape(1, 16 * 128))
    lbl = c(np.asarray(lb_logits_a, f).reshape(2, 16, 128).transpose(0, 2, 1))
    gnm = c(np.asarray(gnorm_a, f)[0].reshape(16, 128).T)
    lvg = c(np.asarray(lnv_g_b, f)[0].reshape(1, 2048))
    lvb = c(np.asarray(lnv_b_b, f)[0].reshape(1, 2048))
    bss = c(np.tile(bsr[:, 0:4], (1, 16)).reshape(1, 16 * 64))
    lng = c(np.asarray(ln_g, f).reshape(2, 1, 1024))
    lnb = c(np.asarray(ln_b, f).reshape(2, 1, 1024))
    xp_all = np.asarray(x_prompt, f)
    xs_all = np.asarray(x_sample, f).reshape(NCORES, NSTOK, D)
    st = np.asarray(state_hgrn, f)[0].reshape(NCORES, NS, H, 128, 128)
    in_maps = []
    for cid in range(NCORES):
        s0 = c(st[cid].transpose(1, 2, 0, 3).reshape(H, 128, NS * 128))
        in_maps.append(dict(xp=c(xp_all[cid]), xs=c(xs_all[cid]), s0=s0, wia=wia, woa=woa, wuz=wuz, wv=wv, wob=wob,
                            wst=wst, wsm=wsm, bs=bs, bss=bss, lbl=lbl, gnm=gnm, lvg=lvg, lvb=lvb, lvbc=lvb,
                            lng=lng, lnb=lnb))
    if "nc" not in _NC_CACHE:
        _NC_CACHE["nc"] = build_nc()
    nc = _NC_CACHE["nc"]
    res = run_bass_kernel_spmd(nc, in_maps, core_ids=list(range(NCORES)))
    r = res.results
    y_prompt = np.stack([r[i]["yp"] for i in range(NCORES)], 0).astype(f)
    y_sample = np.stack([r[i]["ys"] for i in range(NCORES)], 0).reshape(128, 4, D).astype(f)
    hp = np.stack([r[i]["spo"] for i in range(NCORES)], 0)[None].astype(f)
    hs = np.stack([r[i]["sso"].reshape(H, 128, NS, 128).transpose(2, 0, 1, 3) for i in range(NCORES)], 0)
    hs = hs.reshape(1, 128, H, 128, 128).astype(f)
    cv = np.stack([r[i]["cvo"] for i in range(NCORES)], 0).reshape(1, 128, 4, 2048).astype(f)
    return (y_prompt, y_sample, hp, hs, cv)
```

```python
import math
from contextlib import ExitStack

import numpy as np
import concourse.bass as bass
import concourse.mybir as mybir
from concourse.bass_utils import run_bass_kernel_spmd

F32 = mybir.dt.float32
BF16 = mybir.dt.bfloat16
AF = mybir.ActivationFunctionType
ALU = mybir.AluOpType

NCORES = 8
D = 1024
SEQ = 2048
NS = 16
NSTOK = 64
H = 16
ALPHA = 4.0 ** 0.25
EPS = 1e-5
LN_HALF = math.log(0.5)
GC = 0.7978845608028654


def MM(t, out, lhsT, rhs, start, stop):
    return t.matmul(out, lhsT=lhsT, rhs=rhs, start=start, stop=stop, skip_group_check=True)


class Buf:
    def __init__(self, name, t):
        self.name = name
        self.t = t
        self.lw = None
        self.rd = {}
        self.dsem = None
        self.dcnt = 0

    def __getitem__(self, k):
        return self.t[k]


class Alias(Buf):
    def __init__(self, base, t):
        self.base = base
        self.name = base.name
        self.t = t

    lw = property(lambda s: s.base.lw, lambda s, v: setattr(s.base, "lw", v))
    rd = property(lambda s: s.base.rd, lambda s, v: setattr(s.base, "rd", v))
    dsem = property(lambda s: s.base.dsem, lambda s, v: setattr(s.base, "dsem", v))
    dcnt = property(lambda s: s.base.dcnt, lambda s, v: setattr(s.base, "dcnt", v))


class Sched:
    EPOCH = 3000

    def __init__(self, nc, es):
        self.nc = nc
        self.es = es
        self.eng = {"pe": nc.tensor, "act": nc.scalar, "dve": nc.vector, "pool": nc.gpsimd, "sp": nc.sync}
        self.sem = {}
        self.cnt = {}
        self.nsem = 0
        for e in ("pe", "act", "dve", "pool"):
            self.sem[e] = self._newsem(e)
            self.cnt[e] = 0
        self.known = {e: {} for e in self.eng}
        self.pend_r = []
        self.pend_w = []
        self.dma_toks = []
        self.out_toks = []
        self.ninst = 0

    def _newsem(self, tag):
        self.nsem += 1
        return self.es.enter_context(self.nc.semaphore("s%s%d" % (tag, self.nsem)))

    def _wait(self, e, toks):
        k = self.known[e]
        for tok in toks:
            if tok is None:
                continue
            sem, val, src = tok
            if e == "pe" and src == "pe":
                continue
            key = id(sem)
            if k.get(key, 0) >= val:
                continue
            self.eng[e].wait_ge(sem, val)
            k[key] = val

    def _deps(self, reads, writes):
        toks = []
        for b in reads:
            toks.append(b.lw)
        for b in writes:
            toks.append(b.lw)
            toks.extend(b.rd.values())
        return toks

    def _mark(self, tok, reads, writes):
        key = id(tok[0])
        for b in reads:
            b.rd[key] = tok
        for b in writes:
            b.lw = tok
            b.rd = {}

    def op(self, e, fn, reads=(), writes=(), inc=True):
        self._wait(e, self._deps(reads, writes))
        ins = fn(self.eng[e])
        self.ninst += 1
        if e == "pe" and not inc:
            self.pend_r.extend(reads)
            self.pend_w.extend(writes)
            return
        if self.cnt[e] >= self.EPOCH:
            self.sem[e] = self._newsem(e)
            self.cnt[e] = 0
        self.cnt[e] += 1
        ins.then_inc(self.sem[e], 1)
        tok = (self.sem[e], self.cnt[e], e)
        if e == "pe":
            reads = list(reads) + self.pend_r
            writes = list(writes) + self.pend_w
            self.pend_r, self.pend_w = [], []
        self._mark(tok, reads, writes)

    def dma(self, q, out, in_, sbuf, load, is_out=False):
        if load:
            self._wait(q, self._deps((), (sbuf,)))
        else:
            self._wait(q, self._deps((sbuf,), ()))
        if sbuf.dsem is None:
            sbuf.dsem = self._newsem("d")
        ins = self.eng[q].dma_start(out=out, in_=in_)
        sbuf.dcnt += 16
        ins.then_inc(sbuf.dsem, 16)
        tok = (sbuf.dsem, sbuf.dcnt, "dma")
        if load:
            self._mark(tok, (), (sbuf,))
        else:
            self._mark(tok, (sbuf,), ())
        self.dma_toks.append(tok)
        if is_out:
            self.out_toks.append(tok)

    def barrier(self):
        assert not self.pend_r and not self.pend_w
        toks = [(self.sem[e], self.cnt[e], "x") for e in ("pe", "act", "dve", "pool") if self.cnt[e] > 0]
        last = {}
        for t in self.dma_toks:
            last[id(t[0])] = t
        toks += list(last.values())
        for e in self.eng:
            k = self.known[e]
            for sem, val, _ in toks:
                if k.get(id(sem), 0) >= val:
                    continue
                self.eng[e].wait_ge(sem, val)
                k[id(sem)] = val
        self.dma_toks = []

    def finish(self):
        self._wait("sp", self.out_toks)


def build_nc():
    nc = bass.Bass("TRN2", target_bir_lowering=False)

    def din(name, shape):
        return nc.dram_tensor(name, list(shape), F32, kind="ExternalInput").ap()

    def dout(name, shape):
        return nc.dram_tensor(name, list(shape), F32, kind="ExternalOutput").ap()

    xp = din("xp", [SEQ, D])
    xs = din("xs", [NSTOK, D])
    s0 = din("s0", [H, 128, NS * 128])
    wia = din("wia", [H, 128, 8 * 512])
    woa = din("woa", [128, 16 * 1024])
    wuz = din("wuz", [16, 128, 8 * 256])
    wv = din("wv", [128, 8 * 2048])
    wob = din("wob", [128, 16 * 1024])
    wst = din("wst", [128, 16 * 128])
    wsm = din("wsm", [64, 16 * 64])
    bs = din("bs", [1, 16 * 128])
    bss = din("bss", [1, 16 * 64])
    lbl = din("lbl", [2, 128, 16])
    gnm = din("gnm", [128, 16])
    lvg = din("lvg", [1, 2048])
    lvb = din("lvb", [1, 2048])
    lvbc = din("lvbc", [1, 2048])
    lng = din("lng", [2, 1, 1024])
    lnb = din("lnb", [2, 1, 1024])
    yp = dout("yp", [SEQ, D])
    ys = dout("ys", [NSTOK, D])
    spo = dout("spo", [H, 128, 128])
    sso = dout("sso", [H, 128, NS * 128])
    cvo = dout("cvo", [NSTOK, 2048])

    with ExitStack() as es:
        S = Sched(nc, es)

        def sb(name, shape, dt):
            return Buf(name, es.enter_context(nc.sbuf_tensor(name, list(shape), dt)))

        XT = [sb("XT%d" % i, [128, 8, 512 if i < 2 else 64], BF16) for i in range(3)]
        OT = [[sb("OT%d_%d" % (h, i), [128, 512 if i < 2 else 64], BF16) for i in range(3)] for h in range(H)]
        WH = [sb("WH%d" % i, [128, 8 * 512], BF16) for i in range(2)]
        WBIG = sb("WBIG", [128, 16 * 1024], BF16)
        SC = [sb("SC%d" % h, [128, 128], F32) for h in range(H)]
        XIN = [sb("XIN%d" % i, [128, 1024], F32) for i in range(2)]
        identF = sb("identF", [128, 128], F32)
        identB = sb("identB", [128, 128], BF16)
        onesB = sb("onesB", [128, 128], BF16)
        maskP = sb("maskP", [128, 128], F32)
        maskS = sb("maskS", [64, 64], F32)
        maskU = sb("maskU", [128, 128], F32)
        rowsel = sb("rowsel", [64, 16], F32)
        rm = sb("rm", [128, 512], F32)
        rmS = sb("rmS", [128, 64], F32)
        lbc = sb("lbc", [128, 6 * 16], F32)
        gnc = sb("gnc", [128, 16], F32)
        WST = sb("WST", [128, 16 * 128], BF16)
        WSM = sb("WSM", [64, 16 * 64], BF16)
        LNG = sb("LNG", [128, 1024], F32)
        LNB = sb("LNB", [128, 1024], F32)
        ARENA = sb("ARENA", [128, 18176], F32)
        PS = Buf("PS", es.enter_context(nc.psum_tensor("PS", [128, 4096], F32)))
        PB = [Buf("PB%d" % i, None) for i in range(8)]

        def pbank(i, n=512, p=128, j=0):
            return PS.t[0:p, i * 512 + j:i * 512 + j + n]

        def carve(plan):
            out = {}
            off = 0
            for name, ncol, dt, shape in plan:
                ap = ARENA.t[:, off:off + ncol]
                if dt is BF16:
                    ap = ap.bitcast(BF16)
                if len(shape) == 3:
                    ap = ap.rearrange("p (a b) -> p a b", b=shape[2])
                b = Buf(name, ap)
                out[name] = b
                off += ncol
            assert off <= 18176, off
            return out

        def cast_dma(buf, dst, src):
            n = dst.shape[-1]
            if n > 512 and len(dst.shape) == 2:
                dst = dst.rearrange("p (a b) -> p a b", b=512)
                src = src.rearrange("p (a b) -> p a b", b=512)
            S.dma("pool", dst, src, buf, True)

        cast_dma(WH[0], WH[0][:], wia[0])
        cast_dma(WH[1], WH[1][:], wia[1])

        def P(fn, reads=(), writes=()):
            S.op("pool", fn, reads, writes)

        def V(fn, reads=(), writes=()):
            S.op("dve", fn, reads, writes)

        def A(fn, reads=(), writes=()):
            S.op("act", fn, reads, writes)

        def T(fn, reads=(), writes=(), inc=True):
            S.op("pe", fn, reads, writes, inc)

        P(lambda g: g.memset(identF[:], 0.0), (), (identF,))
        P(lambda g: g.affine_select(out=identF[:], in_=identF[:], pattern=[[-1, 128]], compare_op=ALU.not_equal,
                                    fill=1.0, base=0, channel_multiplier=1), (identF,), (identF,))
        P(lambda g: g.tensor_copy(out=identB[:], in_=identF[:]), (identF,), (identB,))
        P(lambda g: g.memset(onesB[:], 1.0), (), (onesB,))
        P(lambda g: g.memset(maskP[:], 1.0), (), (maskP,))
        P(lambda g: g.affine_select(out=maskP[:], in_=maskP[:], pattern=[[1, 128]], compare_op=ALU.is_ge,
                                    fill=0.0, base=0, channel_multiplier=-1), (maskP,), (maskP,))
        P(lambda g: g.tensor_copy(out=maskU[:], in_=maskP[:]), (maskP,), (maskU,))
        P(lambda g: g.memset(maskP[0:64, 64:128], 0.0), (maskP,), (maskP,))
        P(lambda g: g.memset(maskS[:], 1.0), (), (maskS,))
        mS3 = maskS[:].rearrange("p (a b) -> p a b", b=4)
        P(lambda g: g.affine_select(out=mS3, in_=mS3, pattern=[[4, 16], [1, 4]], compare_op=ALU.is_ge,
                                    fill=0.0, base=0, channel_multiplier=-1), (maskS,), (maskS,))
        P(lambda g: g.affine_select(out=mS3, in_=mS3, pattern=[[-4, 16], [0, 4]], compare_op=ALU.is_ge,
                                    fill=0.0, base=0, channel_multiplier=1), (maskS,), (maskS,))
        P(lambda g: g.memset(rowsel[:], 1.0), (), (rowsel,))
        P(lambda g: g.affine_select(out=rowsel[:], in_=rowsel[:], pattern=[[-4, 16]], compare_op=ALU.is_ge,
                                    fill=0.0, base=0, channel_multiplier=1), (rowsel,), (rowsel,))
        P(lambda g: g.affine_select(out=rowsel[:], in_=rowsel[:], pattern=[[4, 16]], compare_op=ALU.is_ge,
                                    fill=0.0, base=3, channel_multiplier=-1), (rowsel,), (rowsel,))
        P(lambda g: g.memset(rm[:], 1.0), (), (rm,))
        P(lambda g: g.memset(rm[:].rearrange("p (c t) -> p c t", t=64)[:, :, 0:1], 0.0), (rm,), (rm,))
        P(lambda g: g.memset(rmS[:], 1.0), (), (rmS,))
        P(lambda g: g.memset(rmS[:].rearrange("p (c t) -> p c t", t=4)[:, :, 0:1], 0.0), (rmS,), (rmS,))

        S.dma("sp", lbc[:, 0:16], lbl[0], lbc, True)
        S.dma("sp", lbc[:, 16:32], lbl[1], lbc, True)
        S.dma("sp", gnc[:], gnm[:, :], gnc, True)
        cast_dma(WST, WST[:], wst[:, :])
        cast_dma(WSM, WSM[:], wsm[:, :])
        V(lambda v: v.tensor_tensor(out=lbc[:, 32:48], in0=lbc[:, 0:16], in1=lbc[:, 16:32], op=ALU.subtract), (lbc,), (lbc,))
        A(lambda a: a.activation(out=lbc[:, 32:48], in_=lbc[:, 32:48], func=AF.Tanh, scale=0.5), (lbc,), (lbc,))
        V(lambda v: v.tensor_scalar(out=lbc[:, 0:16], in0=lbc[:, 32:48], scalar1=0.25, scalar2=0.75, op0=ALU.mult, op1=ALU.add), (lbc,), (lbc,))
        V(lambda v: v.tensor_scalar(out=lbc[:, 16:32], in0=lbc[:, 32:48], scalar1=-0.25, scalar2=0.25, op0=ALU.mult, op1=ALU.add), (lbc,), (lbc,))
        V(lambda v: v.tensor_scalar(out=lbc[:, 48:64], in0=lbc[:, 32:48], scalar1=0.25, scalar2=-0.25, op0=ALU.mult, op1=ALU.add), (lbc,), (lbc,))
        V(lambda v: v.memset(lbc[:, 64:65], EPS), (lbc,), (lbc,))
        V(lambda v: v.memset(lbc[:, 65:66], LN_HALF), (lbc,), (lbc,))
        V(lambda v: v.memset(lbc[:, 66:67], 0.0), (lbc,), (lbc,))
        V(lambda v: v.memset(lbc[:, 67:68], 4.0 * EPS), (lbc,), (lbc,))
        epsc = lbc[:, 64:65]
        lnhc = lbc[:, 65:66]
        W3 = WST[:].rearrange("p (g t) -> p g t", t=128)
        mP3 = maskP[:]
        for g in range(16):
            P(lambda e, g=g: e.tensor_tensor(out=W3[:, g, :], in0=W3[:, g, :], in1=maskU[:], op=ALU.mult), (WST, maskU), (WST,))
        Wm3 = WSM[:].rearrange("p (g t) -> p g t", t=64)
        for g in range(16):
            P(lambda e, g=g: e.tensor_tensor(out=Wm3[:, g, :], in0=Wm3[:, g, :], in1=maskS[:], op=ALU.mult), (WSM, maskS), (WSM,))
        for h in range(H):
            P(lambda e, h=h: e.memset(SC[h][:], 0.0), (), (SC[h],))

        def x_issue(j, stg):
            rows = 128 if j < 8 else 64
            src = xp[1024 + j * 128:1024 + j * 128 + rows, :] if j < 8 else xs[0:64, :]
            S.dma("sp", stg[0:rows, 0:1024], src, stg, True)

        def x_transpose(j, stg, banks):
            rows = 128 if j < 8 else 64
            bi = j // 4 if j < 8 else 2
            c0_ = (j % 4) * 128 if j < 8 else 0
            for half in range(2):
                bk = banks[half]
                for kc4 in range(4):
                    kc = half * 4 + kc4
                    T(lambda t, kc=kc, kc4=kc4, bk=bk: t.transpose(
                        pbank(bk, rows, 128, kc4 * 128), stg[0:rows, kc * 128:(kc + 1) * 128], identF[0:rows, 0:rows]),
                      (stg, identF), (PB[bk],), inc=(kc4 == 3))
                src = pbank(bk, 512).rearrange("p (a b) -> p a b", b=128)[:, :, 0:rows]
                dst = XT[bi][:, half * 4:half * 4 + 4, c0_:c0_ + rows]
                if half == 0:
                    A(lambda a, src=src, dst=dst: a.activation(out=dst, in_=src, func=AF.Copy), (PB[bk],), (XT[bi],))
                else:
                    V(lambda v, src=src, dst=dst: v.tensor_copy(out=dst, in_=src), (PB[bk],), (XT[bi],))

        def load_x_block(sbi, bi, ntile_rows, xsrc, row0):
            nt = 4 if ntile_rows == 128 else 1
            for ti in range(nt):
                xin = XIN[ti % 2]
                rows = ntile_rows
                S.dma("sp", xin[0:rows, :], xsrc[row0 + ti * 128:row0 + ti * 128 + rows, :], xin, True)
                for half in range(2):
                    bk = 6 + half
                    for kc4 in range(4):
                        kc = half * 4 + kc4
                        T(lambda t, kc=kc, kc4=kc4, bk=bk, xin=xin, rows=rows: t.transpose(
                            pbank(bk, rows, 128, kc4 * 128), xin[0:rows, kc * 128:(kc + 1) * 128], identF[0:rows, 0:rows]),
                          (xin, identF), (PB[bk],), inc=(kc4 == 3))
                    src = pbank(bk, 512).rearrange("p (a b) -> p a b", b=128)[:, :, 0:rows]
                    dst = XT[bi][:, half * 4:half * 4 + 4, ti * 128:ti * 128 + rows]
                    if half == 0:
                        A(lambda a, src=src, dst=dst: a.activation(out=dst, in_=src, func=AF.Copy), (PB[bk],), (XT[bi],))
                    else:
                        V(lambda v, src=src, dst=dst: v.tensor_copy(out=dst, in_=src), (PB[bk],), (XT[bi],))

        def l0_unit(h, bi, W, wk, st):
            samp = bi == 2
            N = 64 if samp else 512
            W3h = W[:].rearrange("p (k c) -> p k c", c=512)
            xt = XT[bi]
            c0 = lbc[:, h:h + 1]
            c1 = lbc[:, 16 + h:17 + h]
            nc1 = lbc[:, 48 + h:49 + h]
            pz, pq, pg, pv = 0, 1, 2, 0

            def proj(bk, sl):
                for kc in range(8):
                    T(lambda t, kc=kc: MM(t, pbank(bk, N), lhsT=W3h[:, kc, sl * 128:(sl + 1) * 128], rhs=xt[:, kc, :],
                                               start=(kc == 0), stop=(kc == 7)),
                      (W, xt), (PB[bk],), inc=(kc == 7))
            tz, tq, tg, kn, cum, lf = wk["tz"], wk["tq"], wk["tg"], wk["kn"], wk["cum"], wk["lf"]
            ntile = 1 if samp else 4
            rows = 64 if samp else 128
            csz = 4 if samp else 64
            nch = N // csz
            proj(pz, 1)
            proj(pq, 0)
            proj(pg, 3)
            A(lambda a: a.activation(out=tz[:, 0:N], in_=pbank(pz, N), func=AF.Tanh, scale=0.5), (PB[pz],), (tz,))
            A(lambda a: a.activation(out=tq[:, 0:N], in_=pbank(pq, N), func=AF.Silu), (PB[pq],), (tq,))
            A(lambda a: a.activation(out=tg[:, 0:N], in_=pbank(pg, N), func=AF.Silu), (PB[pg],), (tg,))
            P(lambda e: e.tensor_scalar(out=kn[:, 0:N], in0=tz[:, 0:N], scalar1=nc1, scalar2=c1, op0=ALU.mult, op1=ALU.add),
              (tz, lbc), (kn,))
            A(lambda a: a.activation(out=lf[:, 0:N], in_=tz[:, 0:N], func=AF.Ln, scale=c1, bias=c0), (tz, lbc), (lf,))
            if samp:
                S0h = [st["S0a"], st["S0c"]]
            yield
            for ti in range(ntile):
                for kc in range(8):
                    T(lambda t, kc=kc, ti=ti: MM(t, pbank(pv, 128, rows, ti * 128), lhsT=xt[:, kc, ti * 128:ti * 128 + rows],
                                                     rhs=W3h[:, kc, 256:384], start=(ti == 0 and kc == 0), stop=(kc == 7)),
                      (W, xt), (PB[pv],), inc=(ti == ntile - 1 and kc == 7))
            rmask = rmS if samp else rm
            V(lambda v: v.tensor_tensor_scan(out=cum[:, 0:N], data0=rmask[:, 0:N], data1=lf[:, 0:N], initial=0.0,
                                             op0=ALU.mult, op1=ALU.add), (rmask, lf), (cum,))
            Vt = wk["V"]
            A(lambda a: a.activation(out=Vt[0:rows, 0:ntile * 128], in_=pbank(pv, ntile * 128, rows), func=AF.Copy),
              (PB[pv],), (Vt,))
            E1 = wk["E1"]
            A(lambda a: a.activation(out=E1[:, 0:N], in_=cum[:, 0:N], func=AF.Exp), (cum,), (E1,))
            A(lambda a: a.activation(out=lf[:, 0:N], in_=cum[:, 0:N], func=AF.Exp, scale=-1.0), (cum,), (lf,))
            gl = wk["gl"]
            clv = cum[:, 0:N].rearrange("p (c t) -> p c t", t=csz)[:, :, csz - 1:csz]
            A(lambda a: a.activation(out=gl[:, 0:nch].rearrange("p (c o) -> p c o", o=1), in_=clv, func=AF.Exp), (cum,), (gl,))
            Ab = wk["A"]
            P(lambda v: v.tensor_tensor(out=Ab[:, 0:N], in0=tq[:, 0:N], in1=E1[:, 0:N], op=ALU.mult), (tq, E1), (Ab,))
            Bb = wk["B"]
            P(lambda e: e.tensor_tensor(out=Bb[:, 0:N], in0=kn[:, 0:N], in1=lf[:, 0:N], op=ALU.mult), (kn, lf), (Bb,))
            KT = wk["KT"]
            P(lambda e: e.tensor_tensor(out=KT[:, 0:N].rearrange("p (c t) -> p c t", t=csz),
                                        in0=Bb[:, 0:N].rearrange("p (c t) -> p c t", t=csz),
                                        in1=gl[:, 0:nch].rearrange("p (c o) -> p c o", o=1).to_broadcast([128, nch, csz]),
                                        op=ALU.mult), (Bb, gl), (KT,))
            yield
            pkt, psc, po, pu0, pu1 = 3, 4, 5, 6, 7
            for ti in range(ntile):
                T(lambda t, ti=ti: MM(t, pbank(pkt, 128, rows, ti * 128), lhsT=KT[:, ti * 128:ti * 128 + rows], rhs=identB[:],
                                            start=(ti == 0), stop=True), (KT, identB), (PB[pkt],), inc=(ti == ntile - 1))
            Ke = st["Ke"]
            A(lambda a: a.activation(out=Ke[0:rows, 0:ntile * 128], in_=pbank(pkt, ntile * 128, rows), func=AF.Copy),
              (PB[pkt],), (Ke,))
            for ti in range(ntile):
                T(lambda t, ti=ti: MM(t, pbank(psc, rows, rows, ti * 128), lhsT=Bb[:, ti * 128:ti * 128 + rows],
                                            rhs=Ab[:, ti * 128:ti * 128 + rows], start=(ti == 0), stop=True),
                  (Ab, Bb), (PB[psc],), inc=(ti == ntile - 1))
            Sc = st["Sc"]
            if samp:
                V(lambda v: v.tensor_tensor(out=Sc[0:64, 0:64], in0=pbank(psc, 64, 64), in1=maskS[:], op=ALU.mult),
                  (PB[psc], maskS), (Sc,))
            else:
                V(lambda v: v.tensor_tensor(out=Sc[:, 0:512].rearrange("p (a b) -> p a b", b=128),
                                            in0=pbank(psc, 512).rearrange("p (a b) -> p a b", b=128),
                                            in1=maskP[:].rearrange("p (o b) -> p o b", o=1).to_broadcast([128, 4, 128]),
                                            op=ALU.mult), (PB[psc], maskP), (Sc,))
            if samp:
                S0b = st["S0b"]
                VM = st["VM"]
                V(lambda e: e.tensor_tensor(
                    out=VM[0:64, :].rearrange("p (i v) -> p i v", v=128),
                    in0=Vt[0:64, 0:128].rearrange("p (o v) -> p o v", o=1).to_broadcast([64, 16, 128]),
                    in1=rowsel[:, 0:16].rearrange("p (i o) -> p i o", o=1).to_broadcast([64, 16, 128]),
                    op=ALU.mult), (Vt, rowsel), (VM,))
                for hf in range(2):
                    S0 = S0h[hf]
                    A(lambda a, hf=hf: a.activation(out=S0b[:, hf * 1024:(hf + 1) * 1024], in_=S0[:, :], func=AF.Copy), (S0,), (S0b,))
                    P(lambda e, hf=hf: e.tensor_tensor(
                        out=S0[:, :].rearrange("p (i v) -> p i v", v=128), in0=S0[:, :].rearrange("p (i v) -> p i v", v=128),
                        in1=gl[:, hf * 8:hf * 8 + 8].rearrange("p (i o) -> p i o", o=1).to_broadcast([128, 8, 128]),
                        op=ALU.mult), (S0, gl), (S0,))
            yield
            if not samp:
                for n in range(8):
                    ti, hf = n // 2, n % 2
                    bk = pu0 + hf
                    T(lambda t, n=n, ti=ti, hf=hf, bk=bk: MM(t,
                        pbank(bk, 128, 128, ti * 128), lhsT=Ke[hf * 64:hf * 64 + 64, ti * 128:ti * 128 + 128],
                        rhs=Vt[hf * 64:hf * 64 + 64, ti * 128:ti * 128 + 128], start=(ti == 0), stop=True),
                      (Ke, Vt), (PB[bk],), inc=(ti == 3))
                Sall = st["Sall"]
                if bi == 0:
                    V(lambda v: v.tensor_copy(out=Sall[:, 0, :], in_=SC[h][:]), (SC[h],), (Sall,))
                else:
                    V(lambda v: v.tensor_copy(out=Sall[:, 0, :], in_=Sall[:, 8, :]), (Sall,), (Sall,))
                for n in range(8):
                    bk = pu0 + (n % 2)
                    V(lambda v, n=n, bk=bk: v.scalar_tensor_tensor(out=Sall[:, n + 1, :], in0=Sall[:, n, :], scalar=gl[:, n:n + 1],
                                                                in1=pbank(bk, 128, 128, (n // 2) * 128), op0=ALU.mult, op1=ALU.add),
                      (Sall, gl, PB[bk]), (Sall,))
                Sbf = st["Sbf"]
                V(lambda v: v.tensor_copy(out=Sbf[:, 0:8, :], in_=Sall[:, 0:8, :]), (Sall,), (Sbf,))
                yield
                if bi == 1:
                    P(lambda e: e.tensor_copy(out=SC[h][:], in_=Sall[:, 8, :]), (Sall,), (SC[h],))
                for ti in range(4):
                    T(lambda t, ti=ti: MM(t, pbank(po, 128, 128, ti * 128), lhsT=Vt[:, ti * 128:ti * 128 + 128],
                                                rhs=Sc[:, ti * 128:ti * 128 + 128], start=(ti == 0), stop=False),
                      (Vt, Sc), (PB[po],), inc=False)
                for n in range(8):
                    T(lambda t, n=n: MM(t, pbank(po, 64, 128, n * 64), lhsT=Sbf[:, n, :], rhs=Ab[:, n * 64:n * 64 + 64],
                                              start=False, stop=True), (Sbf, Ab), (PB[po],), inc=(n == 7))
            else:
                S0b = st["S0b"]
                VM = st["VM"]
                for hf in range(2):
                    S0 = S0h[hf]
                    for j in range(2):
                        bk = pu0 + j
                        T(lambda t, j=j, bk=bk, hf=hf: MM(t, pbank(bk, 512), lhsT=Ke[0:64, 0:128],
                                                        rhs=VM[0:64, hf * 1024 + j * 512:hf * 1024 + (j + 1) * 512],
                                                        start=True, stop=True), (Ke, VM), (PB[bk],), inc=True)
                    V(lambda v: v.tensor_tensor(out=S0[:, :], in0=S0[:, :], in1=PS.t[:, pu0 * 512:pu0 * 512 + 1024], op=ALU.add),
                      (S0, PB[pu0], PB[pu1]), (S0,))
                    S.dma("sp", sso[h, :, hf * 1024:(hf + 1) * 1024], S0[:, :], S0, False, is_out=True)
                yield
                T(lambda t: MM(t, pbank(po, 64), lhsT=Vt[0:64, 0:128], rhs=Sc[0:64, 0:64], start=True, stop=False),
                  (Vt, Sc), (PB[po],), inc=False)
                for i in range(16):
                    T(lambda t, i=i: MM(t, pbank(po, 4, 128, i * 4), lhsT=S0b[:, i * 128:(i + 1) * 128], rhs=Ab[:, i * 4:i * 4 + 4],
                                              start=False, stop=True), (S0b, Ab), (PB[po],), inc=(i == 15))
            osq = st["osq"]
            A(lambda a: a.activation(out=osq[:, 0:N], in_=pbank(po, N), func=AF.Square), (PB[po],), (osq,))
            yield
            T(lambda t: MM(t, pbank(psc, N), lhsT=onesB[:], rhs=osq[:, 0:N], start=True, stop=True), (onesB, osq), (PB[psc],))
            rs = st["rs"]
            A(lambda a: a.activation(out=rs[:, 0:N], in_=pbank(psc, N), func=AF.Ln, scale=1.0 / 128.0, bias=epsc), (PB[psc], lbc), (rs,))
            A(lambda a: a.activation(out=rs[:, 0:N], in_=rs[:, 0:N], func=AF.Exp, scale=-0.5), (rs,), (rs,))
            t1 = st["t1"]
            V(lambda v: v.scalar_tensor_tensor(out=t1[:, 0:N], in0=pbank(po, N), scalar=gnc[:, h:h + 1], in1=rs[:, 0:N],
                                               op0=ALU.mult, op1=ALU.mult), (PB[po], gnc, rs), (t1,))
            ot = OT[h][bi]
            P(lambda v: v.tensor_tensor(out=ot[:, 0:N], in0=t1[:, 0:N], in1=tg[:, 0:N], op=ALU.mult), (t1, tg), (ot,))

        def ln_tile(*a, **k):
            for _ in ln_gen(*a, **k):
                pass

        def ln_gen(y, rows, ncol, stt, gam=None, bet=None, eng_gb="pool", epsap=None):
            yb, yap = y
            nchk = ncol // 512
            V(lambda v: [v.bn_stats(out=stt["bst"][0:rows, c * 6:(c + 1) * 6], in_=yap[:, c * 512:(c + 1) * 512]) for c in range(nchk)][-1],
              (yb,), (stt["bst"],))
            V(lambda v: v.bn_aggr(out=stt["mv"][0:rows, 0:2], in_=stt["bst"][0:rows, 0:nchk * 6]),
              (stt["bst"],), (stt["mv"],))
            mv = stt["mv"]
            yield
            A(lambda a: a.activation(out=mv[0:rows, 2:3], in_=mv[0:rows, 1:2], func=AF.Ln, scale=1.0, bias=(epsc if epsap is None else epsap)[0:rows, :]), (mv, lbc), (mv,))
            A(lambda a: a.activation(out=mv[0:rows, 2:3], in_=mv[0:rows, 2:3], func=AF.Exp, scale=-0.5), (mv,), (mv,))
            yield
            V(lambda v: v.scalar_tensor_tensor(out=mv[0:rows, 3:4], in0=mv[0:rows, 0:1], scalar=-1.0, in1=mv[0:rows, 2:3],
                                               op0=ALU.mult, op1=ALU.mult), (mv,), (mv,))
            A(lambda a: a.activation(out=yap, in_=yap, func=AF.Identity, scale=mv[0:rows, 2:3], bias=mv[0:rows, 3:4]), (yb, mv), (yb,))
            if gam is not None:
                S.op(eng_gb, lambda e: e.tensor_tensor(out=yap, in0=yap, in1=gam[0:rows, 0:ncol], op=ALU.mult), (yb, gam), (yb,))
                S.op(eng_gb, lambda e: e.tensor_tensor(out=yap, in0=yap, in1=bet[0:rows, 0:ncol], op=ALU.add), (yb, bet), (yb,))

        for sbi in range(2):
            nblk = 2 if sbi == 0 else 3
            plan = []
            for i in range(2):
                for nm in ("tz", "tq", "kn", "cum", "E1", "lf"):
                    plan.append(("%s%d" % (nm, i), 512, F32, [128, 512]))
                for nm in ("A", "B", "KT", "V"):
                    plan.append(("%s%d" % (nm, i), 256, BF16, [128, 512]))
                plan.append(("gl%d" % i, 16, F32, [128, 16]))
            plan += [("tgA", 512, F32, [128, 512]), ("tgB", 512, F32, [128, 512]), ("tgC", 512, F32, [128, 512]),
                     ("Ke", 256, BF16, [128, 512]), ("Sc", 256, BF16, [128, 512]), ("ScB", 256, BF16, [128, 512]),
                     ("Sall", 9 * 128, F32, [128, 9, 128]), ("Sbf", 512, BF16, [128, 8, 128]),
                     ("osq", 256, BF16, [128, 512]), ("rs", 512, F32, [128, 512]), ("t1", 512, F32, [128, 512]),
                     ("S0a", 1024, F32, [128, 1024]), ("S0c", 1024, F32, [128, 1024]), ("S0b", 1024, BF16, [128, 2048]), ("VM", 1024, BF16, [128, 2048])]
            ar = carve(plan)
            wks = [{k[:-1]: v for k, v in ar.items() if k.endswith(str(i)) and k[:-1] in ("tz", "tq", "kn", "cum", "E1", "lf", "A", "B", "KT", "V", "gl")}
                   for i in range(2)]
            st = ar
            if sbi == 0:
                load_x_block(sbi, 0, 128, xp, 0)
            units = [(h, bi) for h in range(H) for bi in range(2)]
            gens = {}
            tgs = [ar["tgA"], ar["tgB"], ar["tgC"]]

            def step(g):
                if g is not None:
                    next(g, None)

            S.dma("sp", LNG[:], lng[0, 0, :].partition_broadcast(128), LNG, True)
            S.dma("sp", LNB[:], lnb[0, 0, :].partition_broadcast(128), LNB, True)
            for i in range(len(units) + 2):
                if i < len(units):
                    h, bi = units[i]
                    if bi == 0 and 1 <= h and h + 1 < H:
                        cast_dma(WH[(h + 1) % 2], WH[(h + 1) % 2][:], wia[h + 1])
                    if nblk == 3 and bi == 0 and h == H - 1:
                        cast_dma(WH[0], WH[0][:], wia[0])
                    if 2 <= i < 10:
                        j = i - 2
                        cast_dma(WBIG, WBIG[:, j * 2048:(j + 1) * 2048], woa[:, j * 2048:(j + 1) * 2048])
                    wk = dict(wks[i % 2])
                    wk["tg"] = tgs[i % 3]
                    st_i = dict(st)
                    st_i["Sc"] = st["Sc"] if i % 2 == 0 else st["ScB"]
                    gens[i] = l0_unit(h, bi, WH[h % 2], wk, st_i)
                    step(gens[i])
                    if sbi == 0 and i == 0:
                        load_x_block(sbi, 1, 128, xp, 512)
                step(gens.get(i - 1))
                step(gens.get(i - 2))
                if i - 2 in gens:
                    hh, bb = units[i - 2]
                    if sbi == 1 and bb == 1:
                        S.dma("sp", spo[hh], SC[hh][:], SC[hh], False, is_out=True)
                step(gens.get(i))
                step(gens.get(i - 1))
                step(gens.get(i - 2))
                if nblk == 3 and i == len(units) - 1:
                    cast_dma(WH[1], WH[1][:], wia[1])
            S.barrier()
            if nblk == 3:
                plan = []
                for k in range(3):
                    for nm in ("tz", "tq", "kn", "cum", "E1", "lf", "tg", "rs", "t1"):
                        plan.append(("%s%d" % (nm, k), 64, F32, [128, 64]))
                    for nm in ("A", "B", "KT", "Sc", "osq"):
                        plan.append(("%s%d" % (nm, k), 32, BF16, [128, 64]))
                    plan += [("V%d" % k, 64, BF16, [128, 128]), ("Ke%d" % k, 64, BF16, [128, 128]), ("gl%d" % k, 16, F32, [128, 16]),
                             ("S0a%d" % k, 1024, F32, [128, 1024]), ("S0c%d" % k, 1024, F32, [128, 1024]),
                             ("S0b%d" % k, 1024, BF16, [128, 2048]), ("VM%d" % k, 1024, BF16, [128, 2048])]
                ars = carve(plan)
                sets = [{k_[:-1]: v for k_, v in ars.items() if k_.endswith(str(k))} for k in range(3)]
                gens = {}
                for i in range(H + 2):
                    if i < H:
                        h = i
                        if 1 <= h and h + 1 < H:
                            cast_dma(WH[(h + 1) % 2], WH[(h + 1) % 2][:], wia[h + 1])
                        sd = sets[i % 3]
                        for hf in range(2):
                            S0x = sd["S0a"] if hf == 0 else sd["S0c"]
                            S.dma("sp", S0x[:, :], s0[h, :, hf * 1024:(hf + 1) * 1024], S0x, True)
                        gens[i] = l0_unit(h, 2, WH[h % 2], sd, sd)
                        step(gens[i])
                    step(gens.get(i - 1))
                    step(gens.get(i - 2))
                    step(gens.get(i))
                    step(gens.get(i - 1))
                    step(gens.get(i - 2))
                S.barrier()

            ntl = 8 if sbi == 0 else 9
            plan = [("X1", 9 * 1024, F32, [128, 9, 1024]),
                    ("VG", 2048, F32, [128, 2048]), ("NG", 1024, BF16, [128, 2048]), ("GB", 1024, BF16, [128, 2048]),
                    ("VGb", 2048, F32, [128, 2048]),
                    ("bst", 32, F32, [128, 32]), ("mv", 8, F32, [128, 8]),
                    ("RR", 1024, BF16, [128, 2048]), ("RRS", 512, BF16, [128, 1024]), ("LLg", 1024, BF16, [128, 2048])]
            ar = carve(plan)
            X1 = [Buf("X1_%d" % i, ar["X1"].t[:, i, :]) for i in range(9)]
            Tsets = [[Alias(base, base.t[:, j * 512:(j + 1) * 512]) for j in range(3)]
                     for k, base in enumerate((ar["VGb"], ar["VG"]))]
            RR, RRS, LLg = ar["RR"], ar["RRS"], ar["LLg"]
            for g in range(2):
                cast_dma(WH[g], WH[g][:, 0:2048], wuz[g])

            def tile_info(ti):
                if ti < 8:
                    return 128, ti // 4, (ti % 4) * 128
                return 64, 2, 0

            def outproj_ln(ti, src_bufs, resid, out_ap, l, split=False):
                rows, bi, c0_ = tile_info(ti)
                WB3 = WBIG[:].rearrange("p (e f) -> p e f", f=1024)
                WS1 = WBIG[:, 0:8192].rearrange("p (e f) -> p e f", f=512)
                WS0 = [WH[k][:, 0:4096].rearrange("p (e f) -> p e f", f=512) for k in range(2)]
                db = (ti % 2) * 2
                for nh in range(2):
                    for e in range(16):
                        if not split:
                            rhs, wb = WB3[:, e, nh * 512:(nh + 1) * 512], WBIG
                        elif nh == 0:
                            rhs, wb = WS0[e // 8][:, e % 8, :], WH[e // 8]
                        else:
                            rhs, wb = WS1[:, e, :], WBIG
                        T(lambda t, e=e, nh=nh, rhs=rhs: MM(t, pbank(db + nh, 512, rows), lhsT=src_bufs[e][bi][:, c0_:c0_ + rows],
                                                       rhs=rhs, start=(e == 0), stop=(e == 15)),
                          (src_bufs[e][bi], wb), (PB[db + nh],), inc=(e == 15))
                rb, rap = resid
                ob, oap = out_ap
                V(lambda v: v.scalar_tensor_tensor(out=oap, in0=rap, scalar=ALPHA, in1=PS.t[0:rows, db * 512:db * 512 + 1024], op0=ALU.mult, op1=ALU.add),
                  (rb, PB[db], PB[db + 1]), (ob,))
                ln_tile((ob, oap), rows, 1024, ar, LNG, LNB)

            def tail_transposes(ti):
                rows, bi, c0_ = tile_info(ti)
                for half in range(2):
                    bk = 4 + half + 2 * (ti % 2)
                    for kc4 in range(4):
                        kc = half * 4 + kc4
                        T(lambda t, kc=kc, kc4=kc4, bk=bk: t.transpose(
                            pbank(bk, rows, 128, kc4 * 128), X1[ti][0:rows, kc * 128:(kc + 1) * 128], identF[0:rows, 0:rows]),
                          (X1[ti], identF), (PB[bk],), inc=(kc4 == 3))
                    src = pbank(bk, 512).rearrange("p (a b) -> p a b", b=128)[:, :, 0:rows]
                    dst = XT[bi][:, half * 4:half * 4 + 4, c0_:c0_ + rows]
                    if half == 0:
                        A(lambda a, src=src, dst=dst: a.activation(out=dst, in_=src, func=AF.Copy), (PB[bk],), (XT[bi],))
                    else:
                        V(lambda v, src=src, dst=dst: v.tensor_copy(out=dst, in_=src), (PB[bk],), (XT[bi],))

            for ti in range(ntl):
                rows, bi, c0_ = tile_info(ti)
                xin = XIN[ti % 2]
                if ti < 8:
                    S.dma("sp", xin[0:rows, :], xp[sbi * 1024 + ti * 128:sbi * 1024 + ti * 128 + rows, :], xin, True)
                else:
                    S.dma("sp", xin[0:rows, :], xs[0:rows, :], xin, True)
                outproj_ln(ti, OT, (xin, xin[0:rows, :]), (X1[ti], X1[ti][0:rows, :]), 0)
                if ti > 0:
                    tail_transposes(ti - 1)
            tail_transposes(ntl - 1)

            GB = ar["GB"]
            cast_dma(GB, GB[:], lvg[0, :].partition_broadcast(128))
            V(lambda e: e.tensor_scalar(out=GB[:], in0=GB[:], scalar1=0.5, scalar2=None, op0=ALU.mult), (GB,), (GB,))
            S.dma("sp", LNG[:], lng[1, 0, :].partition_broadcast(128), LNG, True)
            S.dma("sp", LNB[:], lnb[1, 0, :].partition_broadcast(128), LNB, True)
            for j in range(4):
                T(lambda t, j=j: MM(t, pbank(4 + j, 512, 2), lhsT=onesB[:, 0:2], rhs=WST[:, j * 512:(j + 1) * 512], start=True, stop=True),
                  (onesB, WST), (PB[4 + j],))
            A(lambda a: a.activation(out=RR[0:1, :], in_=PS.t[0:1, 2048:4096], func=AF.Copy), (PB[4], PB[5], PB[6], PB[7]), (RR,))
            cast_dma(RR, RR[1:2, :], bs[0:1, :])
            for j in range(2):
                T(lambda t, j=j: MM(t, pbank(4 + j, 512, 2), lhsT=onesB[0:64, 0:2], rhs=WSM[:, j * 512:(j + 1) * 512], start=True, stop=True),
                  (onesB, WSM), (PB[4 + j],))
            A(lambda a: a.activation(out=RRS[0:1, :], in_=PS.t[0:1, 2048:3072], func=AF.Copy), (PB[4], PB[5]), (RRS,))
            cast_dma(RRS, RRS[1:2, :], bss[0:1, :])
            P(lambda e: e.memset(LLg[0:2, :], 1.0), (), (LLg,))
            cast_dma(LLg, LLg[0:1, :], lvbc[0:1, :])
            V(lambda e: e.tensor_scalar(out=LLg[0:2, :], in0=LLg[0:2, :], scalar1=0.5, scalar2=None, op0=ALU.mult), (LLg,), (LLg,))
            it = 0
            for g in range(16):
                W = WH[g % 2]
                Wg = W[:, 0:2048].rearrange("p (k c) -> p k c", c=256)
                for bi in range(nblk):
                    N = 512 if bi < 2 else 64
                    xt = XT[bi]
                    pu_, pz_ = (it % 2) * 2, (it % 2) * 2 + 1
                    T1, T2, T3 = Tsets[it % 2]
                    it += 1
                    for sl, bk in ((0, pu_), (1, pz_)):
                        for kc in range(8):
                            T(lambda t, kc=kc, sl=sl, bk=bk: MM(t, pbank(bk, N), lhsT=Wg[:, kc, sl * 128:(sl + 1) * 128], rhs=xt[:, kc, :],
                                                                   start=(kc == 0), stop=(kc == 7)), (W, xt), (PB[bk],), inc=(kc == 7))
                    A(lambda a: a.activation(out=T2[:, 0:N], in_=pbank(pu_, N), func=AF.Gelu_apprx_tanh), (PB[pu_],), (T2,))
                    A(lambda a: a.activation(out=T3[:, 0:N], in_=pbank(pz_, N), func=AF.Tanh, scale=0.5), (PB[pz_],), (T3,))
                    V(lambda v: v.scalar_tensor_tensor(out=T3[:, 0:N], in0=T3[:, 0:N], scalar=1.0, in1=pbank(pz_, N), op0=ALU.add, op1=ALU.mult), (T3, PB[pz_]), (T3,))
                    ot = OT[g][bi]
                    P(lambda e: e.tensor_tensor(out=ot[:, 0:N], in0=T2[:, 0:N], in1=T3[:, 0:N], op=ALU.mult), (T2, T3), (ot,))
                if g + 2 < 16:
                    cast_dma(W, W[:, 0:2048], wuz[g + 2])
                if g < 8:
                    cast_dma(WBIG, WBIG[:, g * 2048:(g + 1) * 2048], wv[:, g * 2048:(g + 1) * 2048])
            WV3 = WBIG[:, 0:8 * 2048].rearrange("p (k c) -> p k c", c=2048)
            VGs = [ar["VG"], ar["VGb"]]
            NG = ar["NG"]
            TT = XIN[1]
            GBF = XIN[0]

            def v_proj(ti):
                rows, bi, c0_ = tile_info(ti)
                xt = XT[bi]
                VG = VGs[ti % 2]
                for nb in range(4):
                    bk = nb
                    TT = XIN[1 - (nb % 2)]
                    T4 = TT[0:rows, 0:512]
                    T5 = TT[0:rows, 512:1024]
                    for kc in range(8):
                        T(lambda t, kc=kc, nb=nb, bk=bk: MM(t, pbank(bk, 512, rows), lhsT=xt[:, kc, c0_:c0_ + rows],
                                                               rhs=WV3[:, kc, nb * 512:(nb + 1) * 512], start=(kc == 0), stop=(kc == 7)),
                          (WBIG, xt), (PB[bk],), inc=(kc == 7))
                    pv_ = pbank(bk, 512, rows)
                    vg = VG[0:rows, nb * 512:(nb + 1) * 512]
                    A(lambda a, pv_=pv_, vg=vg: a.activation(out=vg, in_=pv_, func=AF.Gelu_apprx_tanh), (PB[bk],), (VG,))
                    yield

            def v_ln(ti):
                rows, bi, c0_ = tile_info(ti)
                VG = VGs[ti % 2]
                for _ in ln_gen((VG, VG[0:rows, :]), rows, 2048, ar):
                    yield
                yield
                P(lambda e: e.tensor_tensor(out=NG[0:rows, :], in0=VG[0:rows, :], in1=GB[0:rows, :], op=ALU.mult), (VG, GB), (NG,))
                if ti == 8:
                    GF, BF = GBF[0:64, 0:512], GBF[0:64, 512:1024]
                    for nb in range(4):
                        S.dma("sp", GF, lvg[0, nb * 512:(nb + 1) * 512].partition_broadcast(64), GBF, True)
                        S.dma("sp", BF, lvb[0, nb * 512:(nb + 1) * 512].partition_broadcast(64), GBF, True)
                        P(lambda e, nb=nb: e.tensor_tensor(out=GF, in0=VG[0:64, nb * 512:(nb + 1) * 512], in1=GF, op=ALU.mult), (VG, GBF), (GBF,))
                        P(lambda e: e.tensor_tensor(out=GF, in0=GF, in1=BF, op=ALU.add), (GBF,), (GBF,))
                        S.dma("sp", cvo[:, nb * 512:(nb + 1) * 512], GF, GBF, False, is_out=True)

            def v_mix(ti):
                rows, bi, c0_ = tile_info(ti)
                tw = 128 if ti < 8 else 64
                Wsrc = WST if ti < 8 else WSM
                Rsrc = RR if ti < 8 else RRS
                for g in range(16):
                    bk = 4 + (g * tw) // 512
                    off = (g * tw) % 512
                    T(lambda t, g=g, bk=bk, off=off: MM(t,
                        pbank(bk, tw, 128, off), lhsT=NG[0:rows, g * 128:(g + 1) * 128], rhs=Wsrc[0:rows, g * tw:(g + 1) * tw],
                        start=(off == 0), stop=False), (NG, Wsrc), (PB[bk],), inc=False)
                for g in range(16):
                    bk = 4 + (g * tw) // 512
                    off = (g * tw) % 512
                    T(lambda t, g=g, bk=bk, off=off: MM(t,
                        pbank(bk, tw, 128, off), lhsT=LLg[0:2, g * 128:(g + 1) * 128], rhs=Rsrc[0:2, g * tw:(g + 1) * tw],
                        start=False, stop=True), (LLg, Rsrc), (PB[bk],), inc=(off + tw == 512 or g == 15))

            def v_mul(ti):
                rows, bi, c0_ = tile_info(ti)
                tw = 128 if ti < 8 else 64
                for g in range(16):
                    bk = 4 + (g * tw) // 512
                    off = (g * tw) % 512
                    ot = OT[g][bi]
                    V(lambda v, g=g, bk=bk, off=off, ot=ot: v.tensor_tensor(out=ot[:, c0_:c0_ + tw], in0=ot[:, c0_:c0_ + tw],
                                                                          in1=pbank(bk, tw, 128, off), op=ALU.mult), (ot, PB[bk]), (ot,))

            def run(g):
                for _ in g:
                    pass

            def nxt(g):
                if g is not None:
                    next(g, None)

            wob3 = wob[:, :].rearrange("p (e f) -> p e f", f=1024)
            if sbi == 1:
                for k in range(2):
                    S.dma("pool", WH[k][:, 0:4096].rearrange("p (e f) -> p e f", f=512), wob3[:, k * 8:(k + 1) * 8, 0:512], WH[k], True)
            run(v_proj(0))
            run(v_proj(1))
            run(v_ln(0))
            for ti in range(ntl):
                v_mix(ti)
                gl_ = v_ln(ti + 1) if ti + 1 < ntl else None
                gp_ = v_proj(ti + 2) if ti + 2 < ntl else None
                if gl_ is not None:
                    run(gl_)
                v_mul(ti)
                if gp_ is not None:
                    run(gp_)
                    if ti + 2 == ntl - 1:
                        if sbi == 0:
                            for j in range(4):
                                cast_dma(WBIG, WBIG[:, j * 4096:(j + 1) * 4096], wob[:, j * 4096:(j + 1) * 4096])
                        else:
                            for j in range(2):
                                S.dma("pool", WBIG[:, j * 4096:(j + 1) * 4096].rearrange("p (e f) -> p e f", f=512),
                                      wob3[:, j * 8:(j + 1) * 8, 512:1024], WBIG, True)
            stgs = [ar["VG"], ar["VGb"]]
            if sbi == 0:
                x_issue(0, stgs[0])
                x_issue(1, stgs[1])
                cast_dma(WH[0], WH[0][:], wia[0])
                cast_dma(WH[1], WH[1][:], wia[1])
            for ti in range(ntl):
                rows, bi, c0_ = tile_info(ti)
                yst = XIN[ti % 2]
                outproj_ln(ti, OT, (X1[ti], X1[ti][0:rows, :]), (yst, yst[0:rows, :]), 1, split=(sbi == 1))
                if sbi == 0:
                    x_transpose(ti, stgs[ti % 2], (4 + 2 * (ti % 2), 5 + 2 * (ti % 2)))
                    if ti + 2 < 9:
                        x_issue(ti + 2, stgs[ti % 2])
                if ti < 8:
                    S.dma("sp", yp[sbi * 1024 + ti * 128:sbi * 1024 + ti * 128 + rows, :], yst[0:rows, :], yst, False, is_out=True)
                else:
                    S.dma("sp", ys[0:rows, :], yst[0:rows, :], yst, False, is_out=True)
            if sbi == 0:
                x_transpose(8, stgs[0], (4, 5))
                S.barrier()
        S.finish()
        print("instructions:", S.ninst, "sems:", S.nsem)
    return nc


_NC_CACHE = {}


def kernel(x_prompt, x_sample, state_hgrn, w_in_a, lb_logits_a, gnorm_a, w_out_a, w_in_b, lnv_g_b, lnv_b_b,
           w_s_b, b_s_b, w_out_b, ln_g, ln_b):
    f = np.float32
    c = np.ascontiguousarray
    wia = c(np.asarray(w_in_a, f)[0].reshape(8, 128, 4, 16, 128).transpose(3, 1, 0, 2, 4).reshape(16, 128, 8 * 512))
    woa = c(np.asarray(w_out_a, f)[0].reshape(16, 128, 1024).transpose(1, 0, 2).reshape(128, 16 * 1024))
    wb = np.asarray(w_in_b, f)[0]
    wuz = c(np.stack([wb[:, 0:2048], wb[:, 4096:6144]], 0).reshape(2, 8, 128, 16, 128).transpose(3, 2, 1, 0, 4).reshape(16, 128, 8 * 256))
    wv = c(wb[:, 2048:4096].reshape(8, 128, 2048).transpose(1, 0, 2).reshape(128, 8 * 2048))
    wob = c(np.asarray(w_out_b, f)[0].reshape(16, 128, 1024).transpose(1, 0, 2).reshape(128, 16 * 1024))
    ws = np.asarray(w_s_b, f)[0]
    wst = c(ws.transpose(2, 0, 1).reshape(128, 16 * 128))
    w4 = ws[:, 0:4, 0:4].transpose(2, 0, 1)
    wsm = c(np.tile(w4[None, :, :, None, :], (16, 1, 1, 16, 1)).reshape(64, 16 * 64))
    bsr = np.asarray(b_s_b, f)[0]
    bs = c(bsr.reshape(1, 16 * 128))
    lbl = c(np.asarray(lb_logits_a, f).reshape(2, 16, 128).transpose(0, 2, 1))
    gnm = c(np.asarray(gnorm_a, f)[0].reshape(16, 128).T)
    lvg = c(np.asarray(lnv_g_b, f)[0].reshape(1, 2048))
    lvb = c(np.asarray(lnv_b_b, f)[0].reshape(1, 2048))
    bss = c(np.tile(bsr[:, 0:4], (1, 16)).reshape(1, 16 * 64))
    lng = c(np.asarray(ln_g, f).reshape(2, 1, 1024))
    lnb = c(np.asarray(ln_b, f).reshape(2, 1, 1024))
    xp_all = np.asarray(x_prompt, f)
    xs_all = np.asarray(x_sample, f).reshape(NCORES, NSTOK, D)
    st = np.asarray(state_hgrn, f)[0].reshape(NCORES, NS, H, 128, 128)
    in_maps = []
    for cid in range(NCORES):
        s0 = c(st[cid].transpose(1, 2, 0, 3).reshape(H, 128, NS * 128))
        in_maps.append(dict(xp=c(xp_all[cid]), xs=c(xs_all[cid]), s0=s0, wia=wia, woa=woa, wuz=wuz, wv=wv, wob=wob,
                            wst=wst, wsm=wsm, bs=bs, bss=bss, lbl=lbl, gnm=gnm, lvg=lvg, lvb=lvb, lvbc=lvb,
                            lng=lng, lnb=lnb))
    if "nc" not in _NC_CACHE:
        _NC_CACHE["nc"] = build_nc()
    nc = _NC_CACHE["nc"]
    res = run_bass_kernel_spmd(nc, in_maps, core_ids=list(range(NCORES)))
    r = res.results
    y_prompt = np.stack([r[i]["yp"] for i in range(NCORES)], 0).astype(f)
    y_sample = np.stack([r[i]["ys"] for i in range(NCORES)], 0).reshape(128, 4, D).astype(f)
    hp = np.stack([r[i]["spo"] for i in range(NCORES)], 0)[None].astype(f)
    hs = np.stack([r[i]["sso"].reshape(H, 128, NS, 128).transpose(2, 0, 1, 3) for i in range(NCORES)], 0)
    hs = hs.reshape(1, 128, H, 128, 128).astype(f)
    cv = np.stack([r[i]["cvo"] for i in range(NCORES)], 0).reshape(1, 128, 4, 2048).astype(f)
    return (y_prompt, y_sample, hp, hs, cv)
```

```python
import math
from contextlib import ExitStack

import numpy as np
import concourse.bass as bass
import concourse.mybir as mybir
from concourse.bass_utils import run_bass_kernel_spmd

F32 = mybir.dt.float32
BF16 = mybir.dt.bfloat16
AF = mybir.ActivationFunctionType
ALU = mybir.AluOpType

NCORES = 8
D = 1024
SEQ = 2048
NS = 16
NSTOK = 64
H = 16
ALPHA = 4.0 ** 0.25
EPS = 1e-5
LN_HALF = math.log(0.5)
GC = 0.7978845608028654


def MM(t, out, lhsT, rhs, start, stop):
    return t.matmul(out, lhsT=lhsT, rhs=rhs, start=start, stop=stop, skip_group_check=True)


class Buf:
    def __init__(self, name, t):
        self.name = name
        self.t = t
        self.lw = None
        self.rd = {}
        self.dsem = None
        self.dcnt = 0

    def __getitem__(self, k):
        return self.t[k]


class Alias(Buf):
    def __init__(self, base, t):
        self.base = base
        self.name = base.name
        self.t = t

    lw = property(lambda s: s.base.lw, lambda s, v: setattr(s.base, "lw", v))
    rd = property(lambda s: s.base.rd, lambda s, v: setattr(s.base, "rd", v))
    dsem = property(lambda s: s.base.dsem, lambda s, v: setattr(s.base, "dsem", v))
    dcnt = property(lambda s: s.base.dcnt, lambda s, v: setattr(s.base, "dcnt", v))


class Sched:
    EPOCH = 3000

    def __init__(self, nc, es):
        self.nc = nc
        self.es = es
        self.eng = {"pe": nc.tensor, "act": nc.scalar, "dve": nc.vector, "pool": nc.gpsimd, "sp": nc.sync}
        self.sem = {}
        self.cnt = {}
        self.nsem = 0
        for e in ("pe", "act", "dve", "pool"):
            self.sem[e] = self._newsem(e)
            self.cnt[e] = 0
        self.known = {e: {} for e in self.eng}
        self.pend_r = []
        self.pend_w = []
        self.dma_toks = []
        self.out_toks = []
        self.ninst = 0

    def _newsem(self, tag):
        self.nsem += 1
        return self.es.enter_context(self.nc.semaphore("s%s%d" % (tag, self.nsem)))

    def _wait(self, e, toks):
        k = self.known[e]
        for tok in toks:
            if tok is None:
                continue
            sem, val, src = tok
            if e == "pe" and src == "pe":
                continue
            key = id(sem)
            if k.get(key, 0) >= val:
                continue
            self.eng[e].wait_ge(sem, val)
            k[key] = val

    def _deps(self, reads, writes):
        toks = []
        for b in reads:
            toks.append(b.lw)
        for b in writes:
            toks.append(b.lw)
            toks.extend(b.rd.values())
        return toks

    def _mark(self, tok, reads, writes):
        key = id(tok[0])
        for b in reads:
            b.rd[key] = tok
        for b in writes:
            b.lw = tok
            b.rd = {}

    def op(self, e, fn, reads=(), writes=(), inc=True):
        self._wait(e, self._deps(reads, writes))
        ins = fn(self.eng[e])
        self.ninst += 1
        if e == "pe" and not inc:
            self.pend_r.extend(reads)
            self.pend_w.extend(writes)
            return
        if self.cnt[e] >= self.EPOCH:
            self.sem[e] = self._newsem(e)
            self.cnt[e] = 0
        self.cnt[e] += 1
        ins.then_inc(self.sem[e], 1)
        tok = (self.sem[e], self.cnt[e], e)
        if e == "pe":
            reads = list(reads) + self.pend_r
            writes = list(writes) + self.pend_w
            self.pend_r, self.pend_w = [], []
        self._mark(tok, reads, writes)

    def dma(self, q, out, in_, sbuf, load, is_out=False):
        if load:
            self._wait(q, self._deps((), (sbuf,)))
        else:
            self._wait(q, self._deps((sbuf,), ()))
        if sbuf.dsem is None:
            sbuf.dsem = self._newsem("d")
        ins = self.eng[q].dma_start(out=out, in_=in_)
        sbuf.dcnt += 16
        ins.then_inc(sbuf.dsem, 16)
        tok = (sbuf.dsem, sbuf.dcnt, "dma")
        if load:
            self._mark(tok, (), (sbuf,))
        else:
            self._mark(tok, (sbuf,), ())
        self.dma_toks.append(tok)
        if is_out:
            self.out_toks.append(tok)

    def barrier(self):
        assert not self.pend_r and not self.pend_w
        toks = [(self.sem[e], self.cnt[e], "x") for e in ("pe", "act", "dve", "pool") if self.cnt[e] > 0]
        last = {}
        for t in self.dma_toks:
            last[id(t[0])] = t
        toks += list(last.values())
        for e in self.eng:
            k = self.known[e]
            for sem, val, _ in toks:
                if k.get(id(sem), 0) >= val:
                    continue
                self.eng[e].wait_ge(sem, val)
                k[id(sem)] = val
        self.dma_toks = []

    def finish(self):
        self._wait("sp", self.out_toks)


def build_nc():
    nc = bass.Bass("TRN2", target_bir_lowering=False)

    def din(name, shape):
        return nc.dram_tensor(name, list(shape), F32, kind="ExternalInput").ap()

    def dout(name, shape):
        return nc.dram_tensor(name, list(shape), F32, kind="ExternalOutput").ap()

    xp = din("xp", [SEQ, D])
    xs = din("xs", [NSTOK, D])
    s0 = din("s0", [H, 128, NS * 128])
    wia = din("wia", [H, 128, 8 * 512])
    woa = din("woa", [128, 16 * 1024])
    wuz = din("wuz", [16, 128, 8 * 256])
    wv = din("wv", [128, 8 * 2048])
    wob = din("wob", [128, 16 * 1024])
    wst = din("wst", [128, 16 * 128])
    wsm = din("wsm", [64, 16 * 64])
    bs = din("bs", [1, 16 * 128])
    bss = din("bss", [1, 16 * 64])
    lbl = din("lbl", [2, 128, 16])
    gnm = din("gnm", [128, 16])
    lvg = din("lvg", [1, 2048])
    lvb = din("lvb", [1, 2048])
    lvbc = din("lvbc", [1, 2048])
    lng = din("lng", [2, 1, 1024])
    lnb = din("lnb", [2, 1, 1024])
    yp = dout("yp", [SEQ, D])
    ys = dout("ys", [NSTOK, D])
    spo = dout("spo", [H, 128, 128])
    sso = dout("sso", [H, 128, NS * 128])
    cvo = dout("cvo", [NSTOK, 2048])

    with ExitStack() as es:
        S = Sched(nc, es)

        def sb(name, shape, dt):
            return Buf(name, es.enter_context(nc.sbuf_tensor(name, list(shape), dt)))

        XT = [sb("XT%d" % i, [128, 8, 512 if i < 2 else 64], BF16) for i in range(3)]
        OT = [[sb("OT%d_%d" % (h, i), [128, 512 if i < 2 else 64], BF16) for i in range(3)] for h in range(H)]
        WH = [sb("WH%d" % i, [128, 8 * 512], BF16) for i in range(2)]
        WBIG = sb("WBIG", [128, 16 * 1024], BF16)
        SC = [sb("SC%d" % h, [128, 128], F32) for h in range(H)]
        XIN = [sb("XIN%d" % i, [128, 1024], F32) for i in range(2)]
        identF = sb("identF", [128, 128], F32)
        identB = sb("identB", [128, 128], BF16)
        onesB = sb("onesB", [128, 128], BF16)
        maskP = sb("maskP", [128, 128], F32)
        maskS = sb("maskS", [64, 64], F32)
        maskU = sb("maskU", [128, 128], F32)
        rowsel = sb("rowsel", [64, 16], F32)
        rm = sb("rm", [128, 512], F32)
        rmS = sb("rmS", [128, 64], F32)
        lbc = sb("lbc", [128, 6 * 16], F32)
        gnc = sb("gnc", [128, 16], F32)
        WST = sb("WST", [128, 16 * 128], BF16)
        WSM = sb("WSM", [64, 16 * 64], BF16)
        LNG = sb("LNG", [128, 1024], F32)
        LNB = sb("LNB", [128, 1024], F32)
        ARENA = sb("ARENA", [128, 18176], F32)
        PS = Buf("PS", es.enter_context(nc.psum_tensor("PS", [128, 4096], F32)))
        PB = [Buf("PB%d" % i, None) for i in range(8)]

        def pbank(i, n=512, p=128, j=0):
            return PS.t[0:p, i * 512 + j:i * 512 + j + n]

        def carve(plan):
            out = {}
            off = 0
            for name, ncol, dt, shape in plan:
                ap = ARENA.t[:, off:off + ncol]
                if dt is BF16:
                    ap = ap.bitcast(BF16)
                if len(shape) == 3:
                    ap = ap.rearrange("p (a b) -> p a b", b=shape[2])
                b = Buf(name, ap)
                out[name] = b
                off += ncol
            assert off <= 18176, off
            return out

        def cast_dma(buf, dst, src):
            n = dst.shape[-1]
            if n > 512 and len(dst.shape) == 2:
                dst = dst.rearrange("p (a b) -> p a b", b=512)
                src = src.rearrange("p (a b) -> p a b", b=512)
            S.dma("pool", dst, src, buf, True)

        def P(fn, reads=(), writes=()):
            S.op("pool", fn, reads, writes)

        def V(fn, reads=(), writes=()):
            S.op("dve", fn, reads, writes)

        def A(fn, reads=(), writes=()):
            S.op("act", fn, reads, writes)

        def T(fn, reads=(), writes=(), inc=True):
            S.op("pe", fn, reads, writes, inc)

        P(lambda g: g.memset(identF[:], 0.0), (), (identF,))
        P(lambda g: g.affine_select(out=identF[:], in_=identF[:], pattern=[[-1, 128]], compare_op=ALU.not_equal,
                                    fill=1.0, base=0, channel_multiplier=1), (identF,), (identF,))
        P(lambda g: g.tensor_copy(out=identB[:], in_=identF[:]), (identF,), (identB,))
        P(lambda g: g.memset(onesB[:], 1.0), (), (onesB,))
        P(lambda g: g.memset(maskP[:], 1.0), (), (maskP,))
        P(lambda g: g.affine_select(out=maskP[:], in_=maskP[:], pattern=[[1, 128]], compare_op=ALU.is_ge,
                                    fill=0.0, base=0, channel_multiplier=-1), (maskP,), (maskP,))
        P(lambda g: g.tensor_copy(out=maskU[:], in_=maskP[:]), (maskP,), (maskU,))
        P(lambda g: g.memset(maskP[0:64, 64:128], 0.0), (maskP,), (maskP,))
        P(lambda g: g.memset(maskS[:], 1.0), (), (maskS,))
        mS3 = maskS[:].rearrange("p (a b) -> p a b", b=4)
        P(lambda g: g.affine_select(out=mS3, in_=mS3, pattern=[[4, 16], [1, 4]], compare_op=ALU.is_ge,
                                    fill=0.0, base=0, channel_multiplier=-1), (maskS,), (maskS,))
        P(lambda g: g.affine_select(out=mS3, in_=mS3, pattern=[[-4, 16], [0, 4]], compare_op=ALU.is_ge,
                                    fill=0.0, base=0, channel_multiplier=1), (maskS,), (maskS,))
        P(lambda g: g.memset(rowsel[:], 1.0), (), (rowsel,))
        P(lambda g: g.affine_select(out=rowsel[:], in_=rowsel[:], pattern=[[-4, 16]], compare_op=ALU.is_ge,
                                    fill=0.0, base=0, channel_multiplier=1), (rowsel,), (rowsel,))
        P(lambda g: g.affine_select(out=rowsel[:], in_=rowsel[:], pattern=[[4, 16]], compare_op=ALU.is_ge,
                                    fill=0.0, base=3, channel_multiplier=-1), (rowsel,), (rowsel,))
        P(lambda g: g.memset(rm[:], 1.0), (), (rm,))
        P(lambda g: g.memset(rm[:].rearrange("p (c t) -> p c t", t=64)[:, :, 0:1], 0.0), (rm,), (rm,))
        P(lambda g: g.memset(rmS[:], 1.0), (), (rmS,))
        P(lambda g: g.memset(rmS[:].rearrange("p (c t) -> p c t", t=4)[:, :, 0:1], 0.0), (rmS,), (rmS,))

        S.dma("sp", lbc[:, 0:16], lbl[0], lbc, True)
        S.dma("sp", lbc[:, 16:32], lbl[1], lbc, True)
        S.dma("sp", gnc[:], gnm[:, :], gnc, True)
        cast_dma(WST, WST[:], wst[:, :])
        cast_dma(WSM, WSM[:], wsm[:, :])
        V(lambda v: v.tensor_tensor(out=lbc[:, 32:48], in0=lbc[:, 0:16], in1=lbc[:, 16:32], op=ALU.subtract), (lbc,), (lbc,))
        A(lambda a: a.activation(out=lbc[:, 32:48], in_=lbc[:, 32:48], func=AF.Tanh, scale=0.5), (lbc,), (lbc,))
        V(lambda v: v.tensor_scalar(out=lbc[:, 0:16], in0=lbc[:, 32:48], scalar1=0.25, scalar2=0.75, op0=ALU.mult, op1=ALU.add), (lbc,), (lbc,))
        V(lambda v: v.tensor_scalar(out=lbc[:, 16:32], in0=lbc[:, 32:48], scalar1=-0.25, scalar2=0.25, op0=ALU.mult, op1=ALU.add), (lbc,), (lbc,))
        V(lambda v: v.tensor_scalar(out=lbc[:, 48:64], in0=lbc[:, 32:48], scalar1=0.25, scalar2=-0.25, op0=ALU.mult, op1=ALU.add), (lbc,), (lbc,))
        V(lambda v: v.memset(lbc[:, 64:65], EPS), (lbc,), (lbc,))
        V(lambda v: v.memset(lbc[:, 65:66], LN_HALF), (lbc,), (lbc,))
        V(lambda v: v.memset(lbc[:, 66:67], 0.0), (lbc,), (lbc,))
        V(lambda v: v.memset(lbc[:, 67:68], 4.0 * EPS), (lbc,), (lbc,))
        epsc = lbc[:, 64:65]
        lnhc = lbc[:, 65:66]
        for h in range(H):
            P(lambda e, h=h: e.memset(SC[h][:], 0.0), (), (SC[h],))

        def x_issue(j, stg):
            rows = 128 if j < 8 else 64
            src = xp[1024 + j * 128:1024 + j * 128 + rows, :] if j < 8 else xs[0:64, :]
            S.dma("sp", stg[0:rows, 0:1024], src, stg, True)

        def x_transpose(j, stg, banks):
            rows = 128 if j < 8 else 64
            bi = j // 4 if j < 8 else 2
            c0_ = (j % 4) * 128 if j < 8 else 0
            for half in range(2):
                bk = banks[half]
                for kc4 in range(4):
                    kc = half * 4 + kc4
                    T(lambda t, kc=kc, kc4=kc4, bk=bk: t.transpose(
                        pbank(bk, rows, 128, kc4 * 128), stg[0:rows, kc * 128:(kc + 1) * 128], identF[0:rows, 0:rows]),
                      (stg, identF), (PB[bk],), inc=(kc4 == 3))
                src = pbank(bk, 512).rearrange("p (a b) -> p a b", b=128)[:, :, 0:rows]
                dst = XT[bi][:, half * 4:half * 4 + 4, c0_:c0_ + rows]
                if half == 0:
                    A(lambda a, src=src, dst=dst: a.activation(out=dst, in_=src, func=AF.Copy), (PB[bk],), (XT[bi],))
                else:
                    V(lambda v, src=src, dst=dst: v.tensor_copy(out=dst, in_=src), (PB[bk],), (XT[bi],))

        def load_x_block(sbi, bi, ntile_rows, xsrc, row0):
            nt = 4 if ntile_rows == 128 else 1
            for ti in range(nt):
                xin = XIN[ti % 2]
                rows = ntile_rows
                S.dma("sp", xin[0:rows, :], xsrc[row0 + ti * 128:row0 + ti * 128 + rows, :], xin, True)
                for half in range(2):
                    bk = 6 + half
                    for kc4 in range(4):
                        kc = half * 4 + kc4
                        T(lambda t, kc=kc, kc4=kc4, bk=bk, xin=xin, rows=rows: t.transpose(
                            pbank(bk, rows, 128, kc4 * 128), xin[0:rows, kc * 128:(kc + 1) * 128], identF[0:rows, 0:rows]),
                          (xin, identF), (PB[bk],), inc=(kc4 == 3))
                    src = pbank(bk, 512).rearrange("p (a b) -> p a b", b=128)[:, :, 0:rows]
                    dst = XT[bi][:, half * 4:half * 4 + 4, ti * 128:ti * 128 + rows]
                    if half == 0:
                        A(lambda a, src=src, dst=dst: a.activation(out=dst, in_=src, func=AF.Copy), (PB[bk],), (XT[bi],))
                    else:
                        V(lambda v, src=src, dst=dst: v.tensor_copy(out=dst, in_=src), (PB[bk],), (XT[bi],))

        def l0_unit(h, bi, W, wk, st):
            samp = bi == 2
            N = 64 if samp else 512
            W3h = W[:].rearrange("p (k c) -> p k c", c=512)
            xt = XT[bi]
            c0 = lbc[:, h:h + 1]
            c1 = lbc[:, 16 + h:17 + h]
            nc1 = lbc[:, 48 + h:49 + h]
            pz, pq, pg, pv = 0, 1, 2, 0

            def proj(bk, sl):
                for kc in range(8):
                    T(lambda t, kc=kc: MM(t, pbank(bk, N), lhsT=W3h[:, kc, sl * 128:(sl + 1) * 128], rhs=xt[:, kc, :],
                                               start=(kc == 0), stop=(kc == 7)),
                      (W, xt), (PB[bk],), inc=(kc == 7))
            tz, tq, tg, kn, cum, lf = wk["tz"], wk["tq"], wk["tg"], wk["kn"], wk["cum"], wk["lf"]
            ntile = 1 if samp else 4
            rows = 64 if samp else 128
            csz = 4 if samp else 64
            nch = N // csz
            proj(pz, 1)
            proj(pq, 0)
            proj(pg, 3)
            A(lambda a: a.activation(out=tz[:, 0:N], in_=pbank(pz, N), func=AF.Tanh, scale=0.5), (PB[pz],), (tz,))
            A(lambda a: a.activation(out=tq[:, 0:N], in_=pbank(pq, N), func=AF.Silu), (PB[pq],), (tq,))
            A(lambda a: a.activation(out=tg[:, 0:N], in_=pbank(pg, N), func=AF.Silu), (PB[pg],), (tg,))
            P(lambda e: e.tensor_scalar(out=kn[:, 0:N], in0=tz[:, 0:N], scalar1=nc1, scalar2=c1, op0=ALU.mult, op1=ALU.add),
              (tz, lbc), (kn,))
            A(lambda a: a.activation(out=lf[:, 0:N], in_=tz[:, 0:N], func=AF.Ln, scale=c1, bias=c0), (tz, lbc), (lf,))
            if samp:
                S0h = [st["S0a"], st["S0c"]]
            yield
            for ti in range(ntile):
                for kc in range(8):
                    T(lambda t, kc=kc, ti=ti: MM(t, pbank(pv, 128, rows, ti * 128), lhsT=xt[:, kc, ti * 128:ti * 128 + rows],
                                                     rhs=W3h[:, kc, 256:384], start=(ti == 0 and kc == 0), stop=(kc == 7)),
                      (W, xt), (PB[pv],), inc=(ti == ntile - 1 and kc == 7))
            rmask = rmS if samp else rm
            V(lambda v: v.tensor_tensor_scan(out=cum[:, 0:N], data0=rmask[:, 0:N], data1=lf[:, 0:N], initial=0.0,
                                             op0=ALU.mult, op1=ALU.add), (rmask, lf), (cum,))
            Vt = wk["V"]
            A(lambda a: a.activation(out=Vt[0:rows, 0:ntile * 128], in_=pbank(pv, ntile * 128, rows), func=AF.Copy),
              (PB[pv],), (Vt,))
            E1 = wk["E1"]
            A(lambda a: a.activation(out=E1[:, 0:N], in_=cum[:, 0:N], func=AF.Exp), (cum,), (E1,))
            A(lambda a: a.activation(out=lf[:, 0:N], in_=cum[:, 0:N], func=AF.Exp, scale=-1.0), (cum,), (lf,))
            gl = wk["gl"]
            clv = cum[:, 0:N].rearrange("p (c t) -> p c t", t=csz)[:, :, csz - 1:csz]
            A(lambda a: a.activation(out=gl[:, 0:nch].rearrange("p (c o) -> p c o", o=1), in_=clv, func=AF.Exp), (cum,), (gl,))
            Ab = wk["A"]
            P(lambda v: v.tensor_tensor(out=Ab[:, 0:N], in0=tq[:, 0:N], in1=E1[:, 0:N], op=ALU.mult), (tq, E1), (Ab,))
            Bb = wk["B"]
            P(lambda e: e.tensor_tensor(out=Bb[:, 0:N], in0=kn[:, 0:N], in1=lf[:, 0:N], op=ALU.mult), (kn, lf), (Bb,))
            KT = wk["KT"]
            P(lambda e: e.tensor_tensor(out=KT[:, 0:N].rearrange("p (c t) -> p c t", t=csz),
                                        in0=Bb[:, 0:N].rearrange("p (c t) -> p c t", t=csz),
                                        in1=gl[:, 0:nch].rearrange("p (c o) -> p c o", o=1).to_broadcast([128, nch, csz]),
                                        op=ALU.mult), (Bb, gl), (KT,))
            yield
            pkt, psc, po, pu0, pu1 = 3, 4, 5, 6, 7
            for ti in range(ntile):
                T(lambda t, ti=ti: MM(t, pbank(pkt, 128, rows, ti * 128), lhsT=KT[:, ti * 128:ti * 128 + rows], rhs=identB[:],
                                            start=(ti == 0), stop=True), (KT, identB), (PB[pkt],), inc=(ti == ntile - 1))
            Ke = st["Ke"]
            A(lambda a: a.activation(out=Ke[0:rows, 0:ntile * 128], in_=pbank(pkt, ntile * 128, rows), func=AF.Copy),
              (PB[pkt],), (Ke,))
            for ti in range(ntile):
                T(lambda t, ti=ti: MM(t, pbank(psc, rows, rows, ti * 128), lhsT=Bb[:, ti * 128:ti * 128 + rows],
                                            rhs=Ab[:, ti * 128:ti * 128 + rows], start=(ti == 0), stop=True),
                  (Ab, Bb), (PB[psc],), inc=(ti == ntile - 1))
            Sc = st["Sc"]
            if samp:
                V(lambda v: v.tensor_tensor(out=Sc[0:64, 0:64], in0=pbank(psc, 64, 64), in1=maskS[:], op=ALU.mult),
                  (PB[psc], maskS), (Sc,))
            else:
                V(lambda v: v.tensor_tensor(out=Sc[:, 0:512].rearrange("p (a b) -> p a b", b=128),
                                            in0=pbank(psc, 512).rearrange("p (a b) -> p a b", b=128),
                                            in1=maskP[:].rearrange("p (o b) -> p o b", o=1).to_broadcast([128, 4, 128]),
                                            op=ALU.mult), (PB[psc], maskP), (Sc,))
            if samp:
                S0b = st["S0b"]
                VM = st["VM"]
                V(lambda e: e.tensor_tensor(
                    out=VM[0:64, :].rearrange("p (i v) -> p i v", v=128),
                    in0=Vt[0:64, 0:128].rearrange("p (o v) -> p o v", o=1).to_broadcast([64, 16, 128]),
                    in1=rowsel[:, 0:16].rearrange("p (i o) -> p i o", o=1).to_broadcast([64, 16, 128]),
                    op=ALU.mult), (Vt, rowsel), (VM,))
                for hf in range(2):
                    S0 = S0h[hf]
                    A(lambda a, hf=hf: a.activation(out=S0b[:, hf * 1024:(hf + 1) * 1024], in_=S0[:, :], func=AF.Copy), (S0,), (S0b,))
                    P(lambda e, hf=hf: e.tensor_tensor(
                        out=S0[:, :].rearrange("p (i v) -> p i v", v=128), in0=S0[:, :].rearrange("p (i v) -> p i v", v=128),
                        in1=gl[:, hf * 8:hf * 8 + 8].rearrange("p (i o) -> p i o", o=1).to_broadcast([128, 8, 128]),
                        op=ALU.mult), (S0, gl), (S0,))
            yield
            if not samp:
                for n in range(8):
                    ti, hf = n // 2, n % 2
                    bk = pu0 + hf
                    T(lambda t, n=n, ti=ti, hf=hf, bk=bk: MM(t,
                        pbank(bk, 128, 128, ti * 128), lhsT=Ke[hf * 64:hf * 64 + 64, ti * 128:ti * 128 + 128],
                        rhs=Vt[hf * 64:hf * 64 + 64, ti * 128:ti * 128 + 128], start=(ti == 0), stop=True),
                      (Ke, Vt), (PB[bk],), inc=(ti == 3))
                Sall = st["Sall"]
                if bi == 0:
                    V(lambda v: v.tensor_copy(out=Sall[:, 0, :], in_=SC[h][:]), (SC[h],), (Sall,))
                else:
                    V(lambda v: v.tensor_copy(out=Sall[:, 0, :], in_=Sall[:, 8, :]), (Sall,), (Sall,))
                for n in range(8):
                    bk = pu0 + (n % 2)
                    V(lambda v, n=n, bk=bk: v.scalar_tensor_tensor(out=Sall[:, n + 1, :], in0=Sall[:, n, :], scalar=gl[:, n:n + 1],
                                                                in1=pbank(bk, 128, 128, (n // 2) * 128), op0=ALU.mult, op1=ALU.add),
                      (Sall, gl, PB[bk]), (Sall,))
                Sbf = st["Sbf"]
                V(lambda v: v.tensor_copy(out=Sbf[:, 0:8, :], in_=Sall[:, 0:8, :]), (Sall,), (Sbf,))
                yield
                if bi == 1:
                    P(lambda e: e.tensor_copy(out=SC[h][:], in_=Sall[:, 8, :]), (Sall,), (SC[h],))
                for ti in range(4):
                    T(lambda t, ti=ti: MM(t, pbank(po, 128, 128, ti * 128), lhsT=Vt[:, ti * 128:ti * 128 + 128],
                                                rhs=Sc[:, ti * 128:ti * 128 + 128], start=(ti == 0), stop=False),
                      (Vt, Sc), (PB[po],), inc=False)
                for n in range(8):
                    T(lambda t, n=n: MM(t, pbank(po, 64, 128, n * 64), lhsT=Sbf[:, n, :], rhs=Ab[:, n * 64:n * 64 + 64],
                                              start=False, stop=True), (Sbf, Ab), (PB[po],), inc=(n == 7))
            else:
                S0b = st["S0b"]
                VM = st["VM"]
                for hf in range(2):
                    S0 = S0h[hf]
                    for j in range(2):
                        bk = pu0 + j
                        T(lambda t, j=j, bk=bk, hf=hf: MM(t, pbank(bk, 512), lhsT=Ke[0:64, 0:128],
                                                        rhs=VM[0:64, hf * 1024 + j * 512:hf * 1024 + (j + 1) * 512],
                                                        start=True, stop=True), (Ke, VM), (PB[bk],), inc=True)
                    V(lambda v: v.tensor_tensor(out=S0[:, :], in0=S0[:, :], in1=PS.t[:, pu0 * 512:pu0 * 512 + 1024], op=ALU.add),
                      (S0, PB[pu0], PB[pu1]), (S0,))
                    S.dma("sp", sso[h, :, hf * 1024:(hf + 1) * 1024], S0[:, :], S0, False, is_out=True)
                yield
                T(lambda t: MM(t, pbank(po, 64), lhsT=Vt[0:64, 0:128], rhs=Sc[0:64, 0:64], start=True, stop=False),
                  (Vt, Sc), (PB[po],), inc=False)
                for i in range(16):
                    T(lambda t, i=i: MM(t, pbank(po, 4, 128, i * 4), lhsT=S0b[:, i * 128:(i + 1) * 128], rhs=Ab[:, i * 4:i * 4 + 4],
                                              start=False, stop=True), (S0b, Ab), (PB[po],), inc=(i == 15))
            osq = st["osq"]
            A(lambda a: a.activation(out=osq[:, 0:N], in_=pbank(po, N), func=AF.Square), (PB[po],), (osq,))
            yield
            T(lambda t: MM(t, pbank(psc, N), lhsT=onesB[:], rhs=osq[:, 0:N], start=True, stop=True), (onesB, osq), (PB[psc],))
            rs = st["rs"]
            A(lambda a: a.activation(out=rs[:, 0:N], in_=pbank(psc, N), func=AF.Ln, scale=1.0 / 128.0, bias=epsc), (PB[psc], lbc), (rs,))
            A(lambda a: a.activation(out=rs[:, 0:N], in_=rs[:, 0:N], func=AF.Exp, scale=-0.5), (rs,), (rs,))
            t1 = st["t1"]
            V(lambda v: v.scalar_tensor_tensor(out=t1[:, 0:N], in0=pbank(po, N), scalar=gnc[:, h:h + 1], in1=rs[:, 0:N],
                                               op0=ALU.mult, op1=ALU.mult), (PB[po], gnc, rs), (t1,))
            ot = OT[h][bi]
            P(lambda v: v.tensor_tensor(out=ot[:, 0:N], in0=t1[:, 0:N], in1=tg[:, 0:N], op=ALU.mult), (t1, tg), (ot,))

        def ln_tile(*a, **k):
            for _ in ln_gen(*a, **k):
                pass

        def ln_gen(y, rows, ncol, stt, gam=None, bet=None, eng_gb="pool", epsap=None):
            yb, yap = y
            nchk = ncol // 512
            V(lambda v: [v.bn_stats(out=stt["bst"][0:rows, c * 6:(c + 1) * 6], in_=yap[:, c * 512:(c + 1) * 512]) for c in range(nchk)][-1],
              (yb,), (stt["bst"],))
            V(lambda v: v.bn_aggr(out=stt["mv"][0:rows, 0:2], in_=stt["bst"][0:rows, 0:nchk * 6]),
              (stt["bst"],), (stt["mv"],))
            mv = stt["mv"]
            yield
            A(lambda a: a.activation(out=mv[0:rows, 2:3], in_=mv[0:rows, 1:2], func=AF.Ln, scale=1.0, bias=(epsc if epsap is None else epsap)[0:rows, :]), (mv, lbc), (mv,))
            A(lambda a: a.activation(out=mv[0:rows, 2:3], in_=mv[0:rows, 2:3], func=AF.Exp, scale=-0.5), (mv,), (mv,))
            yield
            V(lambda v: v.scalar_tensor_tensor(out=mv[0:rows, 3:4], in0=mv[0:rows, 0:1], scalar=-1.0, in1=mv[0:rows, 2:3],
                                               op0=ALU.mult, op1=ALU.mult), (mv,), (mv,))
            A(lambda a: a.activation(out=yap, in_=yap, func=AF.Identity, scale=mv[0:rows, 2:3], bias=mv[0:rows, 3:4]), (yb, mv), (yb,))
            if gam is not None:
                S.op(eng_gb, lambda e: e.tensor_tensor(out=yap, in0=yap, in1=gam[0:rows, 0:ncol], op=ALU.mult), (yb, gam), (yb,))
                S.op(eng_gb, lambda e: e.tensor_tensor(out=yap, in0=yap, in1=bet[0:rows, 0:ncol], op=ALU.add), (yb, bet), (yb,))

        for sbi in range(2):
            nblk = 2 if sbi == 0 else 3
            plan = []
            for i in range(2):
                for nm in ("tz", "tq", "kn", "cum", "E1", "lf"):
                    plan.append(("%s%d" % (nm, i), 512, F32, [128, 512]))
                for nm in ("A", "B", "KT", "V"):
                    plan.append(("%s%d" % (nm, i), 256, BF16, [128, 512]))
                plan.append(("gl%d" % i, 16, F32, [128, 16]))
            plan += [("tgA", 512, F32, [128, 512]), ("tgB", 512, F32, [128, 512]), ("tgC", 512, F32, [128, 512]),
                     ("Ke", 256, BF16, [128, 512]), ("Sc", 256, BF16, [128, 512]), ("ScB", 256, BF16, [128, 512]),
                     ("Sall", 9 * 128, F32, [128, 9, 128]), ("Sbf", 512, BF16, [128, 8, 128]),
                     ("osq", 256, BF16, [128, 512]), ("rs", 512, F32, [128, 512]), ("t1", 512, F32, [128, 512]),
                     ("S0a", 1024, F32, [128, 1024]), ("S0c", 1024, F32, [128, 1024]), ("S0b", 1024, BF16, [128, 2048]), ("VM", 1024, BF16, [128, 2048])]
            ar = carve(plan)
            wks = [{k[:-1]: v for k, v in ar.items() if k.endswith(str(i)) and k[:-1] in ("tz", "tq", "kn", "cum", "E1", "lf", "A", "B", "KT", "V", "gl")}
                   for i in range(2)]
            st = ar
            if sbi == 0:
                load_x_block(sbi, 0, 128, xp, 0)
            units = [(h, bi) for h in range(H) for bi in range(2)]
            gens = {}
            tgs = [ar["tgA"], ar["tgB"], ar["tgC"]]

            def step(g):
                if g is not None:
                    next(g, None)

            if sbi == 0:
                cast_dma(WH[0], WH[0][:], wia[0])
                cast_dma(WH[1], WH[1][:], wia[1])
            S.dma("sp", LNG[:], lng[0, 0, :].partition_broadcast(128), LNG, True)
            S.dma("sp", LNB[:], lnb[0, 0, :].partition_broadcast(128), LNB, True)
            for i in range(len(units) + 2):
                if i < len(units):
                    h, bi = units[i]
                    if bi == 0 and 1 <= h and h + 1 < H:
                        cast_dma(WH[(h + 1) % 2], WH[(h + 1) % 2][:], wia[h + 1])
                    if nblk == 3 and bi == 0 and h == H - 1:
                        cast_dma(WH[0], WH[0][:], wia[0])
                    if 2 <= i < 10:
                        j = i - 2
                        cast_dma(WBIG, WBIG[:, j * 2048:(j + 1) * 2048], woa[:, j * 2048:(j + 1) * 2048])
                    wk = dict(wks[i % 2])
                    wk["tg"] = tgs[i % 3]
                    st_i = dict(st)
                    st_i["Sc"] = st["Sc"] if i % 2 == 0 else st["ScB"]
                    gens[i] = l0_unit(h, bi, WH[h % 2], wk, st_i)
                    step(gens[i])
                    if sbi == 0 and i == 0:
                        load_x_block(sbi, 1, 128, xp, 512)
                step(gens.get(i - 1))
                step(gens.get(i - 2))
                if i - 2 in gens:
                    hh, bb = units[i - 2]
                    if sbi == 1 and bb == 1:
                        S.dma("sp", spo[hh], SC[hh][:], SC[hh], False, is_out=True)
                step(gens.get(i))
                step(gens.get(i - 1))
                step(gens.get(i - 2))
                if nblk == 3 and i == len(units) - 1:
                    cast_dma(WH[1], WH[1][:], wia[1])
            S.barrier()
            if nblk == 3:
                plan = []
                for k in range(3):
                    for nm in ("tz", "tq", "kn", "cum", "E1", "lf", "tg", "rs", "t1"):
                        plan.append(("%s%d" % (nm, k), 64, F32, [128, 64]))
                    for nm in ("A", "B", "KT", "Sc", "osq"):
                        plan.append(("%s%d" % (nm, k), 32, BF16, [128, 64]))
                    plan += [("V%d" % k, 64, BF16, [128, 128]), ("Ke%d" % k, 64, BF16, [128, 128]), ("gl%d" % k, 16, F32, [128, 16]),
                             ("S0a%d" % k, 1024, F32, [128, 1024]), ("S0c%d" % k, 1024, F32, [128, 1024]),
                             ("S0b%d" % k, 1024, BF16, [128, 2048]), ("VM%d" % k, 1024, BF16, [128, 2048])]
                ars = carve(plan)
                sets = [{k_[:-1]: v for k_, v in ars.items() if k_.endswith(str(k))} for k in range(3)]
                gens = {}
                for i in range(H + 2):
                    if i < H:
                        h = i
                        if 1 <= h and h + 1 < H:
                            cast_dma(WH[(h + 1) % 2], WH[(h + 1) % 2][:], wia[h + 1])
                        sd = sets[i % 3]
                        for hf in range(2):
                            S0x = sd["S0a"] if hf == 0 else sd["S0c"]
                            S.dma("sp", S0x[:, :], s0[h, :, hf * 1024:(hf + 1) * 1024], S0x, True)
                        gens[i] = l0_unit(h, 2, WH[h % 2], sd, sd)
                        step(gens[i])
                    step(gens.get(i - 1))
                    step(gens.get(i - 2))
                    step(gens.get(i))
                    step(gens.get(i - 1))
                    step(gens.get(i - 2))
                S.barrier()

            ntl = 8 if sbi == 0 else 9
            plan = [("X1", 9 * 1024, F32, [128, 9, 1024]),
                    ("VG", 2048, F32, [128, 2048]), ("NG", 1024, BF16, [128, 2048]), ("GB", 1024, BF16, [128, 2048]),
                    ("VGb", 2048, F32, [128, 2048]),
                    ("bst", 32, F32, [128, 32]), ("mv", 8, F32, [128, 8]),
                    ("RR", 1024, BF16, [128, 2048]), ("RRS", 512, BF16, [128, 1024]), ("LLg", 1024, BF16, [128, 2048])]
            ar = carve(plan)
            X1 = [Buf("X1_%d" % i, ar["X1"].t[:, i, :]) for i in range(9)]
            Tsets = [[Alias(base, base.t[:, j * 512:(j + 1) * 512]) for j in range(3)]
                     for k, base in enumerate((ar["VGb"], ar["VG"]))]
            RR, RRS, LLg = ar["RR"], ar["RRS"], ar["LLg"]
            for g in range(2):
                cast_dma(WH[g], WH[g][:, 0:2048], wuz[g])

            def tile_info(ti):
                if ti < 8:
                    return 128, ti // 4, (ti % 4) * 128
                return 64, 2, 0

            def outproj_ln(ti, src_bufs, resid, out_ap, l, split=False):
                rows, bi, c0_ = tile_info(ti)
                WB3 = WBIG[:].rearrange("p (e f) -> p e f", f=1024)
                WS1 = WBIG[:, 0:8192].rearrange("p (e f) -> p e f", f=512)
                WS0 = [WH[k][:, 0:4096].rearrange("p (e f) -> p e f", f=512) for k in range(2)]
                db = (ti % 2) * 2
                for nh in range(2):
                    for e in range(16):
                        if not split:
                            rhs, wb = WB3[:, e, nh * 512:(nh + 1) * 512], WBIG
                        elif nh == 0:
                            rhs, wb = WS0[e // 8][:, e % 8, :], WH[e // 8]
                        else:
                            rhs, wb = WS1[:, e, :], WBIG
                        T(lambda t, e=e, nh=nh, rhs=rhs: MM(t, pbank(db + nh, 512, rows), lhsT=src_bufs[e][bi][:, c0_:c0_ + rows],
                                                       rhs=rhs, start=(e == 0), stop=(e == 15)),
                          (src_bufs[e][bi], wb), (PB[db + nh],), inc=(e == 15))
                rb, rap = resid
                ob, oap = out_ap
                V(lambda v: v.scalar_tensor_tensor(out=oap, in0=rap, scalar=ALPHA, in1=PS.t[0:rows, db * 512:db * 512 + 1024], op0=ALU.mult, op1=ALU.add),
                  (rb, PB[db], PB[db + 1]), (ob,))
                ln_tile((ob, oap), rows, 1024, ar, LNG, LNB)

            def tail_transposes(ti):
                rows, bi, c0_ = tile_info(ti)
                for half in range(2):
                    bk = 4 + half + 2 * (ti % 2)
                    for kc4 in range(4):
                        kc = half * 4 + kc4
                        T(lambda t, kc=kc, kc4=kc4, bk=bk: t.transpose(
                            pbank(bk, rows, 128, kc4 * 128), X1[ti][0:rows, kc * 128:(kc + 1) * 128], identF[0:rows, 0:rows]),
                          (X1[ti], identF), (PB[bk],), inc=(kc4 == 3))
                    src = pbank(bk, 512).rearrange("p (a b) -> p a b", b=128)[:, :, 0:rows]
                    dst = XT[bi][:, half * 4:half * 4 + 4, c0_:c0_ + rows]
                    if half == 0:
                        A(lambda a, src=src, dst=dst: a.activation(out=dst, in_=src, func=AF.Copy), (PB[bk],), (XT[bi],))
                    else:
                        V(lambda v, src=src, dst=dst: v.tensor_copy(out=dst, in_=src), (PB[bk],), (XT[bi],))

            for ti in range(ntl):
                rows, bi, c0_ = tile_info(ti)
                xin = XIN[ti % 2]
                if ti < 8:
                    S.dma("sp", xin[0:rows, :], xp[sbi * 1024 + ti * 128:sbi * 1024 + ti * 128 + rows, :], xin, True)
                else:
                    S.dma("sp", xin[0:rows, :], xs[0:rows, :], xin, True)
                outproj_ln(ti, OT, (xin, xin[0:rows, :]), (X1[ti], X1[ti][0:rows, :]), 0)
                if ti > 0:
                    tail_transposes(ti - 1)
            tail_transposes(ntl - 1)

            if sbi == 0:
                W3 = WST[:].rearrange("p (g t) -> p g t", t=128)
                mP3 = maskP[:]
                for g in range(16):
                    P(lambda e, g=g: e.tensor_tensor(out=W3[:, g, :], in0=W3[:, g, :], in1=maskU[:], op=ALU.mult), (WST, maskU), (WST,))
                Wm3 = WSM[:].rearrange("p (g t) -> p g t", t=64)
                for g in range(16):
                    P(lambda e, g=g: e.tensor_tensor(out=Wm3[:, g, :], in0=Wm3[:, g, :], in1=maskS[:], op=ALU.mult), (WSM, maskS), (WSM,))
            GB = ar["GB"]
            cast_dma(GB, GB[:], lvg[0, :].partition_broadcast(128))
            V(lambda e: e.tensor_scalar(out=GB[:], in0=GB[:], scalar1=0.5, scalar2=None, op0=ALU.mult), (GB,), (GB,))
            S.dma("sp", LNG[:], lng[1, 0, :].partition_broadcast(128), LNG, True)
            S.dma("sp", LNB[:], lnb[1, 0, :].partition_broadcast(128), LNB, True)
            for j in range(4):
                T(lambda t, j=j: MM(t, pbank(4 + j, 512, 2), lhsT=onesB[:, 0:2], rhs=WST[:, j * 512:(j + 1) * 512], start=True, stop=True),
                  (onesB, WST), (PB[4 + j],))
            A(lambda a: a.activation(out=RR[0:1, :], in_=PS.t[0:1, 2048:4096], func=AF.Copy), (PB[4], PB[5], PB[6], PB[7]), (RR,))
            cast_dma(RR, RR[1:2, :], bs[0:1, :])
            for j in range(2):
                T(lambda t, j=j: MM(t, pbank(4 + j, 512, 2), lhsT=onesB[0:64, 0:2], rhs=WSM[:, j * 512:(j + 1) * 512], start=True, stop=True),
                  (onesB, WSM), (PB[4 + j],))
            A(lambda a: a.activation(out=RRS[0:1, :], in_=PS.t[0:1, 2048:3072], func=AF.Copy), (PB[4], PB[5]), (RRS,))
            cast_dma(RRS, RRS[1:2, :], bss[0:1, :])
            P(lambda e: e.memset(LLg[0:2, :], 1.0), (), (LLg,))
            cast_dma(LLg, LLg[0:1, :], lvbc[0:1, :])
            V(lambda e: e.tensor_scalar(out=LLg[0:2, :], in0=LLg[0:2, :], scalar1=0.5, scalar2=None, op0=ALU.mult), (LLg,), (LLg,))
            it = 0
            for g in range(16):
                W = WH[g % 2]
                Wg = W[:, 0:2048].rearrange("p (k c) -> p k c", c=256)
                for bi in range(nblk):
                    N = 512 if bi < 2 else 64
                    xt = XT[bi]
                    pu_, pz_ = (it % 2) * 2, (it % 2) * 2 + 1
                    T1, T2, T3 = Tsets[it % 2]
                    it += 1
                    for sl, bk in ((0, pu_), (1, pz_)):
                        for kc in range(8):
                            T(lambda t, kc=kc, sl=sl, bk=bk: MM(t, pbank(bk, N), lhsT=Wg[:, kc, sl * 128:(sl + 1) * 128], rhs=xt[:, kc, :],
                                                                   start=(kc == 0), stop=(kc == 7)), (W, xt), (PB[bk],), inc=(kc == 7))
                    A(lambda a: a.activation(out=T2[:, 0:N], in_=pbank(pu_, N), func=AF.Gelu_apprx_tanh), (PB[pu_],), (T2,))
                    A(lambda a: a.activation(out=T3[:, 0:N], in_=pbank(pz_, N), func=AF.Tanh, scale=0.5), (PB[pz_],), (T3,))
                    V(lambda v: v.scalar_tensor_tensor(out=T3[:, 0:N], in0=T3[:, 0:N], scalar=1.0, in1=pbank(pz_, N), op0=ALU.add, op1=ALU.mult), (T3, PB[pz_]), (T3,))
                    ot = OT[g][bi]
                    P(lambda e: e.tensor_tensor(out=ot[:, 0:N], in0=T2[:, 0:N], in1=T3[:, 0:N], op=ALU.mult), (T2, T3), (ot,))
                if g + 2 < 16:
                    cast_dma(W, W[:, 0:2048], wuz[g + 2])
                if g < 8:
                    cast_dma(WBIG, WBIG[:, g * 2048:(g + 1) * 2048], wv[:, g * 2048:(g + 1) * 2048])
            WV3 = WBIG[:, 0:8 * 2048].rearrange("p (k c) -> p k c", c=2048)
            VGs = [ar["VG"], ar["VGb"]]
            NG = ar["NG"]
            TT = XIN[1]
            GBF = XIN[0]

            def v_proj(ti):
                rows, bi, c0_ = tile_info(ti)
                xt = XT[bi]
                VG = VGs[ti % 2]
                for nb in range(4):
                    bk = nb
                    TT = XIN[1 - (nb % 2)]
                    T4 = TT[0:rows, 0:512]
                    T5 = TT[0:rows, 512:1024]
                    for kc in range(8):
                        T(lambda t, kc=kc, nb=nb, bk=bk: MM(t, pbank(bk, 512, rows), lhsT=xt[:, kc, c0_:c0_ + rows],
                                                               rhs=WV3[:, kc, nb * 512:(nb + 1) * 512], start=(kc == 0), stop=(kc == 7)),
                          (WBIG, xt), (PB[bk],), inc=(kc == 7))
                    pv_ = pbank(bk, 512, rows)
                    vg = VG[0:rows, nb * 512:(nb + 1) * 512]
                    A(lambda a, pv_=pv_, vg=vg: a.activation(out=vg, in_=pv_, func=AF.Gelu_apprx_tanh), (PB[bk],), (VG,))
                    yield

            def v_ln(ti):
                rows, bi, c0_ = tile_info(ti)
                VG = VGs[ti % 2]
                for _ in ln_gen((VG, VG[0:rows, :]), rows, 2048, ar):
                    yield
                yield
                P(lambda e: e.tensor_tensor(out=NG[0:rows, :], in0=VG[0:rows, :], in1=GB[0:rows, :], op=ALU.mult), (VG, GB), (NG,))
                if ti == 8:
                    GF, BF = GBF[0:64, 0:512], GBF[0:64, 512:1024]
                    for nb in range(4):
                        S.dma("sp", GF, lvg[0, nb * 512:(nb + 1) * 512].partition_broadcast(64), GBF, True)
                        S.dma("sp", BF, lvb[0, nb * 512:(nb + 1) * 512].partition_broadcast(64), GBF, True)
                        P(lambda e, nb=nb: e.tensor_tensor(out=GF, in0=VG[0:64, nb * 512:(nb + 1) * 512], in1=GF, op=ALU.mult), (VG, GBF), (GBF,))
                        P(lambda e: e.tensor_tensor(out=GF, in0=GF, in1=BF, op=ALU.add), (GBF,), (GBF,))
                        S.dma("sp", cvo[:, nb * 512:(nb + 1) * 512], GF, GBF, False, is_out=True)

            def v_mix(ti):
                rows, bi, c0_ = tile_info(ti)
                tw = 128 if ti < 8 else 64
                Wsrc = WST if ti < 8 else WSM
                Rsrc = RR if ti < 8 else RRS
                for g in range(16):
                    bk = 4 + (g * tw) // 512
                    off = (g * tw) % 512
                    T(lambda t, g=g, bk=bk, off=off: MM(t,
                        pbank(bk, tw, 128, off), lhsT=NG[0:rows, g * 128:(g + 1) * 128], rhs=Wsrc[0:rows, g * tw:(g + 1) * tw],
                        start=(off == 0), stop=False), (NG, Wsrc), (PB[bk],), inc=False)
                for g in range(16):
                    bk = 4 + (g * tw) // 512
                    off = (g * tw) % 512
                    T(lambda t, g=g, bk=bk, off=off: MM(t,
                        pbank(bk, tw, 128, off), lhsT=LLg[0:2, g * 128:(g + 1) * 128], rhs=Rsrc[0:2, g * tw:(g + 1) * tw],
                        start=False, stop=True), (LLg, Rsrc), (PB[bk],), inc=(off + tw == 512 or g == 15))

            def v_mul(ti):
                rows, bi, c0_ = tile_info(ti)
                tw = 128 if ti < 8 else 64
                for g in range(16):
                    bk = 4 + (g * tw) // 512
                    off = (g * tw) % 512
                    ot = OT[g][bi]
                    V(lambda v, g=g, bk=bk, off=off, ot=ot: v.tensor_tensor(out=ot[:, c0_:c0_ + tw], in0=ot[:, c0_:c0_ + tw],
                                                                          in1=pbank(bk, tw, 128, off), op=ALU.mult), (ot, PB[bk]), (ot,))

            def run(g):
                for _ in g:
                    pass

            def nxt(g):
                if g is not None:
                    next(g, None)

            wob3 = wob[:, :].rearrange("p (e f) -> p e f", f=1024)
            if sbi == 1:
                for k in range(2):
                    S.dma("pool", WH[k][:, 0:4096].rearrange("p (e f) -> p e f", f=512), wob3[:, k * 8:(k + 1) * 8, 0:512], WH[k], True)
            run(v_proj(0))
            run(v_proj(1))
            run(v_ln(0))
            for ti in range(ntl):
                v_mix(ti)
                gl_ = v_ln(ti + 1) if ti + 1 < ntl else None
                gp_ = v_proj(ti + 2) if ti + 2 < ntl else None
                if gl_ is not None:
                    run(gl_)
                v_mul(ti)
                if gp_ is not None:
                    run(gp_)
                    if ti + 2 == ntl - 1:
                        if sbi == 0:
                            for j in range(4):
                                cast_dma(WBIG, WBIG[:, j * 4096:(j + 1) * 4096], wob[:, j * 4096:(j + 1) * 4096])
                        else:
                            for j in range(2):
                                S.dma("pool", WBIG[:, j * 4096:(j + 1) * 4096].rearrange("p (e f) -> p e f", f=512),
                                      wob3[:, j * 8:(j + 1) * 8, 512:1024], WBIG, True)
            stgs = [ar["VG"], ar["VGb"]]
            if sbi == 0:
                x_issue(0, stgs[0])
                x_issue(1, stgs[1])
                cast_dma(WH[0], WH[0][:], wia[0])
                cast_dma(WH[1], WH[1][:], wia[1])
            for ti in range(ntl):
                rows, bi, c0_ = tile_info(ti)
                yst = XIN[ti % 2]
                outproj_ln(ti, OT, (X1[ti], X1[ti][0:rows, :]), (yst, yst[0:rows, :]), 1, split=(sbi == 1))
                if sbi == 0:
                    x_transpose(ti, stgs[ti % 2], (4 + 2 * (ti % 2), 5 + 2 * (ti % 2)))
                    if ti + 2 < 9:
                        x_issue(ti + 2, stgs[ti % 2])
                if ti < 8:
                    S.dma("sp", yp[sbi * 1024 + ti * 128:sbi * 1024 + ti * 128 + rows, :], yst[0:rows, :], yst, False, is_out=True)
                else:
                    S.dma("sp", ys[0:rows, :], yst[0:rows, :], yst, False, is_out=True)
            if sbi == 0:
                x_transpose(8, stgs[0], (4, 5))
                S.barrier()
        S.finish()
        print("instructions:", S.ninst, "sems:", S.nsem)
    return nc


_NC_CACHE = {}


def kernel(x_prompt, x_sample, state_hgrn, w_in_a, lb_logits_a, gnorm_a, w_out_a, w_in_b, lnv_g_b, lnv_b_b,
           w_s_b, b_s_b, w_out_b, ln_g, ln_b):
    f = np.float32
    c = np.ascontiguousarray
    wia = c(np.asarray(w_in_a, f)[0].reshape(8, 128, 4, 16, 128).transpose(3, 1, 0, 2, 4).reshape(16, 128, 8 * 512))
    woa = c(np.asarray(w_out_a, f)[0].reshape(16, 128, 1024).transpose(1, 0, 2).reshape(128, 16 * 1024))
    wb = np.asarray(w_in_b, f)[0]
    wuz = c(np.stack([wb[:, 0:2048], wb[:, 4096:6144]], 0).reshape(2, 8, 128, 16, 128).transpose(3, 2, 1, 0, 4).reshape(16, 128, 8 * 256))
    wv = c(wb[:, 2048:4096].reshape(8, 128, 2048).transpose(1, 0, 2).reshape(128, 8 * 2048))
    wob = c(np.asarray(w_out_b, f)[0].reshape(16, 128, 1024).transpose(1, 0, 2).reshape(128, 16 * 1024))
    ws = np.asarray(w_s_b, f)[0]
    wst = c(ws.transpose(2, 0, 1).reshape(128, 16 * 128))
    w4 = ws[:, 0:4, 0:4].transpose(2, 0, 1)
    wsm = c(np.tile(w4[None, :, :, None, :], (16, 1, 1, 16, 1)).reshape(64, 16 * 64))
    bsr = np.asarray(b_s_b, f)[0]
    bs = c(bsr.reshape(1, 16 * 128))
    lbl = c(np.asarray(lb_logits_a, f).reshape(2, 16, 128).transpose(0, 2, 1))
    gnm = c(np.asarray(gnorm_a, f)[0].reshape(16, 128).T)
    lvg = c(np.asarray(lnv_g_b, f)[0].reshape(1, 2048))
    lvb = c(np.asarray(lnv_b_b, f)[0].reshape(1, 2048))
    bss = c(np.tile(bsr[:, 0:4], (1, 16)).reshape(1, 16 * 64))
    lng = c(np.asarray(ln_g, f).reshape(2, 1, 1024))
    lnb = c(np.asarray(ln_b, f).reshape(2, 1, 1024))
    xp_all = np.asarray(x_prompt, f)
    xs_all = np.asarray(x_sample, f).reshape(NCORES, NSTOK, D)
    st = np.asarray(state_hgrn, f)[0].reshape(NCORES, NS, H, 128, 128)
    in_maps = []
    for cid in range(NCORES):
        s0 = c(st[cid].transpose(1, 2, 0, 3).reshape(H, 128, NS * 128))
        in_maps.append(dict(xp=c(xp_all[cid]), xs=c(xs_all[cid]), s0=s0, wia=wia, woa=woa, wuz=wuz, wv=wv, wob=wob,
                            wst=wst, wsm=wsm, bs=bs, bss=bss, lbl=lbl, gnm=gnm, lvg=lvg, lvb=lvb, lvbc=lvb,
                            lng=lng, lnb=lnb))
    if "nc" not in _NC_CACHE:
        _NC_CACHE["nc"] = build_nc()
    nc = _NC_CACHE["nc"]
    res = run_bass_kernel_spmd(nc, in_maps, core_ids=list(range(NCORES)))
    r = res.results
    y_prompt = np.stack([r[i]["yp"] for i in range(NCORES)], 0).astype(f)
    y_sample = np.stack([r[i]["ys"] for i in range(NCORES)], 0).reshape(128, 4, D).astype(f)
    hp = np.stack([r[i]["spo"] for i in range(NCORES)], 0)[None].astype(f)
    hs = np.stack([r[i]["sso"].reshape(H, 128, NS, 128).transpose(2, 0, 1, 3) for i in range(NCORES)], 0)
    hs = hs.reshape(1, 128, H, 128, 128).astype(f)
    cv = np.stack([r[i]["cvo"] for i in range(NCORES)], 0).reshape(1, 128, 4, 2048).astype(f)
    return (y_prompt, y_sample, hp, hs, cv)
```
